# Optimizing a Trainium2 kernel written in Bass

```python
import jax, jax.numpy as jnp
from jax import lax
import numpy as np

D_MODEL = 1024
BATCH = 16
SEQ = 256
DEPTH = 2
DEC_BATCH = 2
DEC_SEQ = 4096
PAST_LEN = 256

GRID_W = 64
N_EVEN = (DEPTH + 1) // 2
N_ODD = DEPTH // 2
EPS = 1e-6
H_A = 4
DH_A = D_MODEL // 8
W_A = H_A * DH_A
CHUNK_A = 128
FORGET_BIAS = 3.0
HQ_B = 8
HKV_B = 2
G_B = HQ_B // HKV_B
DH_B = D_MODEL // 16
W_B = HQ_B * DH_B
QBLOCK = 128
ROPE_THETA = 10000.0
H_C = 16
DH_C = D_MODEL // 16
W_C = H_C * DH_C
WIN_R_MAX = 8
WIN_C = 16
EVEN_SIZES = (W_A, W_A, W_A, W_A, 4 * H_A, W_B, HKV_B * DH_B, HKV_B * DH_B, W_A + W_B)
IN_EVEN = 4 * W_A + 4 * H_A + W_B + 2 * HKV_B * DH_B + W_A + W_B
IN_ODD = 4 * W_C

kernel_name = 'hybrid_mlstm_gqa_natten_diffusion_step'


def offsets(sizes):
    out, acc = [], 0
    for s in sizes[:-1]:
        acc += s
        out.append(acc)
    return out


def rmsnorm(x, g):
    xf = x.astype(jnp.float32)
    y = xf * lax.rsqrt(jnp.mean(xf * xf, axis=-1, keepdims=True) + EPS)
    return (y * g.astype(jnp.float32)).astype(x.dtype)


def modulation(cvec, w, b):
    m = jnp.dot(jax.nn.silu(cvec), w) + b
    return jnp.split(m, 3, axis=-1)


def to_heads(a, n):
    B, T, _ = a.shape
    return a.reshape(B, T, n, -1).transpose(0, 2, 1, 3)


def from_heads(a):
    B, H, T, D = a.shape
    return a.transpose(0, 2, 1, 3).reshape(B, T, H * D)


def axial_rope_tables(n_tok):
    t = jnp.arange(n_tok)
    row = (t // GRID_W).astype(jnp.float32)
    col = (t % GRID_W).astype(jnp.float32)
    quarter = DH_B // 4
    freqs = ROPE_THETA ** (-jnp.arange(quarter, dtype=jnp.float32) / quarter)
    ang_r = row[:, None] * freqs
    ang_c = col[:, None] * freqs
    return (jnp.cos(ang_r), jnp.sin(ang_r), jnp.cos(ang_c), jnp.sin(ang_c))


def rope_half(x, cos, sin):
    x1, x2 = jnp.split(x, 2, axis=-1)
    return jnp.concatenate([x1 * cos - x2 * sin, x1 * sin + x2 * cos], axis=-1)


def axial_rope(x, tables):
    cos_r, sin_r, cos_c, sin_c = (t.astype(x.dtype) for t in tables)
    xr, xc = jnp.split(x, 2, axis=-1)
    return jnp.concatenate([rope_half(xr, cos_r, sin_r), rope_half(xc, cos_c, sin_c)], axis=-1)


def mlstm_scan(q, k, v, ig, fg, C0, n0, m0):
    f32 = jnp.float32
    B, H, T, Dh = q.shape
    L = CHUNK_A
    nc = T // L

    def chunks(a):
        a = a.reshape(B, H, nc, L, *a.shape[3:])
        return jnp.moveaxis(a, 2, 0)

    qc = chunks(q.astype(f32))
    kc = chunks(k.astype(f32) * Dh ** -0.5)
    vc = chunks(v.astype(f32))
    logf = chunks(jax.nn.log_sigmoid(fg.astype(f32)))
    logi = chunks(ig.astype(f32))
    causal = jnp.tril(jnp.ones((L, L), dtype=bool))

    def step(carry, xs):
        C, n, m = carry
        qj, kj, vj, lf, li = xs
        b = jnp.cumsum(lf, axis=-1)
        Dm = jnp.where(causal, b[..., :, None] - b[..., None, :] + li[..., None, :], -jnp.inf)
        m_inter = b + m[..., None]
        m_t = jnp.maximum(m_inter, jnp.max(Dm, axis=-1))
        w_inter = jnp.exp(m_inter - m_t)
        P = jnp.exp(Dm - m_t[..., None]) * jnp.einsum('bhld,bhsd->bhls', qj, kj)
        num = w_inter[..., None] * jnp.einsum('bhvk,bhlk->bhlv', C, qj) + jnp.einsum('bhls,bhsv->bhlv', P, vj)
        den = w_inter * jnp.einsum('bhk,bhlk->bhl', n, qj) + jnp.sum(P, axis=-1)
        h = num / jnp.maximum(jnp.abs(den), jnp.exp(-m_t))[..., None]
        bL = b[..., -1]
        g = bL[..., None] - b + li
        m_new = jnp.maximum(bL + m, jnp.max(g, axis=-1))
        a_st = jnp.exp(bL + m - m_new)
        wk = jnp.exp(g - m_new[..., None])
        C_new = a_st[..., None, None] * C + jnp.einsum('bhs,bhsv,bhsk->bhvk', wk, vj, kj)
        n_new = a_st[..., None] * n + jnp.einsum('bhs,bhsk->bhk', wk, kj)
        return (C_new, n_new, m_new), h

    (C, n, m), hs = lax.scan(step, (C0.astype(f32), n0.astype(f32), m0.astype(f32)), (qc, kc, vc, logf, logi))
    h = jnp.moveaxis(hs, 0, 2).reshape(B, H, T, Dh)
    return h, C, n, m


def block_attend(q, k, v):
    B, KV, G, T, D = q.shape
    nb = T // QBLOCK
    scale = D ** -0.5
    qb = jnp.moveaxis(q.reshape(B, KV, G, nb, QBLOCK, D), 3, 0)

    def blk(qi):
        s = jnp.einsum('bkgqd,bksd->bkgqs', qi, k).astype(jnp.float32) * scale
        p = jax.nn.softmax(s, axis=-1).astype(v.dtype)
        return jnp.einsum('bkgqs,bksd->bkgqd', p, v)

    o = lax.map(blk, qb)
    return jnp.moveaxis(o, 0, 3).reshape(B, KV, G, T, D)


def na_attend(q, k, v, kc, vc, rpb):
    B, H, T, D = q.shape
    rows = T // GRID_W
    wr = min(WIN_R_MAX, rows)
    scale = D ** -0.5
    qg = q.reshape(B, H, rows, GRID_W, D)
    kg = k.reshape(B, H, rows, GRID_W, D)
    vg = v.reshape(B, H, rows, GRID_W, D)
    col = jnp.arange(GRID_W)
    cs = jnp.clip(col - WIN_C // 2, 0, GRID_W - WIN_C)
    col_idx = cs[:, None] + jnp.arange(WIN_C)[None, :]
    rpb_cols = rpb[:, :, col_idx - col[:, None] + WIN_C - 1]

    def row_fn(args):
        r, q_r = args
        rs = jnp.clip(r - wr // 2, 0, rows - wr)
        k_win = lax.dynamic_slice_in_dim(kg, rs, wr, axis=2)[:, :, :, col_idx, :]
        v_win = lax.dynamic_slice_in_dim(vg, rs, wr, axis=2)[:, :, :, col_idx, :]
        bias = rpb_cols[:, rs + jnp.arange(wr) - r + WIN_R_MAX - 1]
        s_loc = (jnp.einsum('bhcd,bhrcwd->bhcrw', q_r, k_win).astype(jnp.float32) * scale
                 + jnp.transpose(bias, (0, 2, 1, 3))[None].astype(jnp.float32))
        s_ctx = jnp.einsum('bhcd,bhpd->bhcp', q_r, kc).astype(jnp.float32) * scale
        p = jax.nn.softmax(jnp.concatenate([s_loc.reshape(B, H, GRID_W, wr * WIN_C), s_ctx], axis=-1), axis=-1)
        p_loc = p[..., :wr * WIN_C].reshape(B, H, GRID_W, wr, WIN_C).astype(v.dtype)
        p_ctx = p[..., wr * WIN_C:].astype(v.dtype)
        return jnp.einsum('bhcrw,bhrcwd->bhcd', p_loc, v_win) + jnp.einsum('bhcp,bhpd->bhcd', p_ctx, vc)

    o = lax.map(row_fn, (jnp.arange(rows), jnp.moveaxis(qg, 2, 0)))
    return jnp.moveaxis(o, 0, 2).reshape(B, H, T, D)


def even_mixer(h, w_in, b_gates, g_hn, g_q, g_k, w_out, st_fwd, st_bwd, kv_ctx, rope):
    B, T, _ = h.shape
    p = jnp.einsum('btd,dn->btn', h, w_in)
    qa, ka, va, oa, gates, qb, kb, vb, z = jnp.split(p, offsets(EVEN_SIZES), axis=-1)
    qa, ka, va, oa = (to_heads(a, H_A) for a in (qa, ka, va, oa))
    g4 = (gates + b_gates).reshape(B, T, 4, H_A).transpose(2, 0, 3, 1)
    ig_f, fg_f, ig_b, fg_b = g4[0], g4[1], g4[2], g4[3]
    rev = lambda a: jnp.flip(a, axis=2)
    h_f, C_f, n_f, m_f = mlstm_scan(qa, ka, va, ig_f, fg_f, *st_fwd)
    h_b, C_b, n_b, m_b = mlstm_scan(rev(qa), rev(ka), rev(va), rev(ig_b), rev(fg_b), *st_bwd)
    ha = rmsnorm(h_f + rev(h_b), g_hn.reshape(H_A, 1, DH_A)) * jax.nn.sigmoid(oa.astype(jnp.float32))
    ha = from_heads(ha.astype(h.dtype))
    qb = rmsnorm(to_heads(qb, HQ_B), g_q)
    kb = rmsnorm(to_heads(kb, HKV_B), g_k)
    vb = to_heads(vb, HKV_B)
    if rope is not None:
        qb = axial_rope(qb, rope)
        kb_pos = axial_rope(kb, rope)
    else:
        kb_pos = kb
    if kv_ctx is None:
        k_all, v_all = kb_pos, vb
    else:
        k_all = jnp.concatenate([kb_pos, kv_ctx[0].astype(kb.dtype)], axis=2)
        v_all = jnp.concatenate([vb, kv_ctx[1].astype(vb.dtype)], axis=2)
    hb = block_attend(qb.reshape(B, HKV_B, G_B, T, DH_B), k_all, v_all).reshape(B, HQ_B, T, DH_B)
    y = jnp.concatenate([ha, from_heads(hb)], axis=-1) * jax.nn.silu(z)
    y = jnp.einsum('btn,nd->btd', y, w_out)
    ctx_tensors = (jnp.stack([C_f, C_b], axis=1), jnp.stack([n_f, n_b], axis=1), jnp.stack([m_f, m_b], axis=1), kb, vb)
    return y, ctx_tensors


def odd_mixer(h, w_in, rpb, w_out, kv_ctx):
    p = jnp.einsum('btd,dn->btn', h, w_in)
    q, k, v, z = jnp.split(p, 4, axis=-1)
    q, k, v = (to_heads(a, H_C) for a in (q, k, v))
    if kv_ctx is None:
        o = block_attend(q[:, :, None], k, v)[:, :, 0]
    else:
        o = na_attend(q, k, v, kv_ctx[0].astype(k.dtype), kv_ctx[1].astype(v.dtype), rpb)
    y = jnp.einsum('btn,nd->btd', from_heads(o) * jax.nn.silu(z), w_out)
    return y, (k, v)


def setup_inputs(seed: int = 0) -> dict:
    key = jax.random.key(seed)
    ks = jax.random.split(key, 32)
    nrm = lambda k, shape, s=1.0: s * jax.random.normal(k, shape, jnp.float32)
    gate_offset = jnp.tile(jnp.repeat(jnp.array([0.0, FORGET_BIAS], jnp.float32), H_A), 2)
    return {
        'x_prompt': nrm(ks[0], (BATCH, SEQ, D_MODEL)),
        'x_sample': nrm(ks[1], (DEC_BATCH, DEC_SEQ, D_MODEL)),
        'state_mlstm_C': nrm(ks[2], (DEC_BATCH, N_EVEN, 2, H_A, DH_A, DH_A), 0.1),
        'state_mlstm_n': nrm(ks[3], (DEC_BATCH, N_EVEN, 2, H_A, DH_A), 0.1),
        'state_mlstm_m': nrm(ks[4], (DEC_BATCH, N_EVEN, 2, H_A)),
        'cache_gqa_k': nrm(ks[5], (DEC_BATCH, N_EVEN, HKV_B, PAST_LEN, DH_B)),
        'cache_gqa_v': nrm(ks[6], (DEC_BATCH, N_EVEN, HKV_B, PAST_LEN, DH_B)),
        'cache_na_k': nrm(ks[7], (DEC_BATCH, N_ODD, H_C, PAST_LEN, DH_C)),
        'cache_na_v': nrm(ks[8], (DEC_BATCH, N_ODD, H_C, PAST_LEN, DH_C)),
        'c': nrm(ks[9], (DEC_BATCH, D_MODEL)),
        'c_ctx': nrm(ks[10], (D_MODEL,)),
        'w_mod': nrm(ks[11], (DEPTH, D_MODEL, 3 * D_MODEL), 0.5 * D_MODEL ** -0.5),
        'b_mod': nrm(ks[12], (DEPTH, 3 * D_MODEL), 0.01),
        'g_pre': 1.0 + nrm(ks[13], (DEPTH, D_MODEL), 0.05),
        'g_post': 1.0 + nrm(ks[14], (DEPTH, D_MODEL), 0.05),
        'w_in_ab': nrm(ks[15], (N_EVEN, D_MODEL, IN_EVEN), D_MODEL ** -0.5),
        'b_gates_ab': gate_offset + nrm(ks[16], (N_EVEN, 4 * H_A), 0.1),
        'g_hnorm_a': 1.0 + nrm(ks[17], (N_EVEN, W_A), 0.05),
        'g_qnorm_b': 1.0 + nrm(ks[18], (N_EVEN, DH_B), 0.05),
        'g_knorm_b': 1.0 + nrm(ks[19], (N_EVEN, DH_B), 0.05),
        'w_out_ab': nrm(ks[20], (N_EVEN, W_A + W_B, D_MODEL), (W_A + W_B) ** -0.5),
        'w_in_c': nrm(ks[21], (N_ODD, D_MODEL, IN_ODD), D_MODEL ** -0.5),
        'rpb_c': nrm(ks[22], (N_ODD, H_C, 2 * WIN_R_MAX - 1, 2 * WIN_C - 1), 0.1),
        'w_out_c': nrm(ks[23], (N_ODD, W_C, D_MODEL), W_C ** -0.5),
    }


def reference(x_prompt, x_sample, state_mlstm_C, state_mlstm_n, state_mlstm_m, cache_gqa_k, cache_gqa_v,
              cache_na_k, cache_na_v, c, c_ctx, w_mod, b_mod, g_pre, g_post, w_in_ab, b_gates_ab, g_hnorm_a,
              g_qnorm_b, g_knorm_b, w_out_ab, w_in_c, rpb_c, w_out_c):
    f32 = jnp.float32
    rope = axial_rope_tables(x_sample.shape[1])
    bp = x_prompt.shape[0]
    xp, xs = x_prompt, x_sample
    new_C, new_n, new_m, new_gk, new_gv, new_nk, new_nv = [], [], [], [], [], [], []
    for layer in range(DEPTH):
        sh_p, sc_p, gt_p = modulation(c_ctx, w_mod[layer], b_mod[layer])
        sh_s, sc_s, gt_s = (a[:, None, :] for a in modulation(c, w_mod[layer], b_mod[layer]))
        hp = rmsnorm(xp, g_pre[layer]) * (1.0 + sc_p) + sh_p
        hs = rmsnorm(xs, g_pre[layer]) * (1.0 + sc_s) + sh_s
        j = layer // 2
        if layer % 2 == 0:
            w = (w_in_ab[j], b_gates_ab[j], g_hnorm_a[j], g_qnorm_b[j], g_knorm_b[j], w_out_ab[j])
            zero = (jnp.zeros((bp, H_A, DH_A, DH_A), f32), jnp.zeros((bp, H_A, DH_A), f32), jnp.zeros((bp, H_A), f32))
            yp, (sC, sn, sm, kb, vb) = even_mixer(hp, *w, zero, zero, None, None)
            st_f = (state_mlstm_C[:, j, 0], state_mlstm_n[:, j, 0], state_mlstm_m[:, j, 0])
            st_b = (state_mlstm_C[:, j, 1], state_mlstm_n[:, j, 1], state_mlstm_m[:, j, 1])
            ys, _ = even_mixer(hs, *w, st_f, st_b, (cache_gqa_k[:, j], cache_gqa_v[:, j]), rope)
            new_C.append(sC)
            new_n.append(sn)
            new_m.append(sm)
            new_gk.append(kb)
            new_gv.append(vb)
        else:
            yp, (kc, vc) = odd_mixer(hp, w_in_c[j], rpb_c[j], w_out_c[j], None)
            ys, _ = odd_mixer(hs, w_in_c[j], rpb_c[j], w_out_c[j], (cache_na_k[:, j], cache_na_v[:, j]))
            new_nk.append(kc)
            new_nv.append(vc)
        xp = xp + gt_p * rmsnorm(yp, g_post[layer])
        xs = xs + gt_s * rmsnorm(ys, g_post[layer])
    dt = x_prompt.dtype
    out_C = jnp.stack(new_C, axis=1).astype(dt)
    out_n = jnp.stack(new_n, axis=1).astype(dt)
    out_m = jnp.stack(new_m, axis=1).astype(dt)
    out_gk = jnp.stack(new_gk, axis=1).astype(dt)
    out_gv = jnp.stack(new_gv, axis=1).astype(dt)
    out_nk = jnp.stack(new_nk, axis=1).astype(dt)
    out_nv = jnp.stack(new_nv, axis=1).astype(dt)
    return (xp, xs, out_C, out_n, out_m, out_gk, out_gv, out_nk, out_nv)
```

```python
import numpy as np
import concourse.bass as bass
import concourse.mybir as mybir
from concourse.bass_utils import run_bass_kernel_spmd
from contextlib import ExitStack

F32 = mybir.dt.float32
BF16 = mybir.dt.bfloat16
AF = mybir.ActivationFunctionType
ALU = mybir.AluOpType
AX = mybir.AxisListType

DBG = 99
NCORES = 8
D = 1024
EPS = 1e-6
NEG = -30000.0
NPC = 4
NOUT = 20
NIN = 12
NCH = NPC + NOUT + NIN
STARTS = [0, 768, 1792, 2560]
C_KA, C_VA, C_KV, C_QA, C_OA, C_QB, C_Z = 0, 512, 1024, 1296, 1808, 2320, 2832
NW0 = 3856


class Buf:
    __slots__ = ("w", "r", "excl")

    def __init__(self, excl=False):
        self.w = None
        self.r = []
        self.excl = excl


class _Cap:
    def __init__(self):
        self.call = None

    def __getattr__(self, name):
        def f(*a, **kw):
            self.call = (name, a, kw)
            return None
        return f


class MK:
    SEM_ROLL = 8000

    def __init__(self, nc, n_dma=40):
        self.nc = nc
        self.es = ExitStack()
        self.eng = {"pe": nc.tensor, "act": nc.scalar, "dve": nc.vector, "pool": nc.gpsimd, "sp": nc.sync}
        self.sem = {}
        self.cnt = {}
        self.seen = {e: {} for e in self.eng}
        self.nsem = 0
        for e in self.eng:
            self._newsem(e)
        n_sw = 16
        self.dma_sems = [self.es.enter_context(nc.semaphore("dq%d" % i)) for i in range(n_dma + n_sw)]
        self.dma_val = [0] * (n_dma + n_sw)
        self.dma_rng = {"hw": (0, n_dma), "sw": (n_dma, n_sw)}
        self.dma_i = {"hw": 0, "sw": 0}
        self.n_inst = {e: 0 for e in self.eng}
        self.last = {}
        self.rec = None
        self.gid = None
        self.gctr = 0

    def _newsem(self, e):
        self.nsem += 1
        self.sem[e] = self.es.enter_context(self.nc.semaphore("s_%s_%d" % (e, self.nsem)))
        self.cnt[e] = 0

    def _waits(self, e, reads, writes, attach=False):
        E = self.eng[e]
        toks = []
        for b in reads:
            if b.w is not None:
                toks.append(b.w)
            if b.excl:
                toks.extend(t for t in b.r if t[2] != e)
        for b in writes:
            if b.w is not None:
                toks.append(b.w)
            toks.extend(b.r)
        seen = self.seen[e]
        need = {}
        for (sem, val, src) in toks:
            if src == e and e == "pe":
                continue
            k = id(sem)
            if seen.get(k, 0) >= val:
                continue
            if k not in need or need[k][1] < val:
                need[k] = (sem, val)
        items = list(need.items())
        for k, (sem, val) in items:
            seen[k] = val
        if attach and items:
            for k, (sem, val) in items[:-1]:
                E.wait_ge(sem, val)
                self.n_inst[e] += 1
            return items[-1][1]
        for k, (sem, val) in items:
            E.wait_ge(sem, val)
            self.n_inst[e] += 1
        return None

    def _done(self, tok, reads, writes):
        for b in reads:
            b.r.append(tok)
            if len(b.r) > 48:
                mx = {}
                for t in b.r:
                    k = id(t[0])
                    if k not in mx or mx[k][1] < t[1]:
                        mx[k] = t
                b.r = list(mx.values())
        for b in writes:
            b.w = tok
            b.r = []

    def group(self):
        mk = self

        class _G:
            def __enter__(self_):
                mk.gctr += 1
                self_.old = mk.gid
                mk.gid = mk.gctr

            def __exit__(self_, *a):
                mk.gid = self_.old
        return _G()

    def record(self, f, *a):
        old = self.rec
        self.rec = []
        f(*a)
        ops = self.rec
        self.rec = old
        return ops

    def replay(self, tasks):
        assert self.rec is None
        idx = [0] * len(tasks)
        left = sum(len(t) for t in tasks)
        while left:
            k = min((k for k in range(len(tasks)) if idx[k] < len(tasks[k])),
                    key=lambda k: (idx[k] + 0.5) / len(tasks[k]))
            g0 = tasks[k][idx[k]][-1]
            while True:
                o = tasks[k][idx[k]]
                idx[k] += 1
                left -= 1
                if o[0] == "op":
                    _, e, (name, a, kw), r, w, _g = o
                    self.op(e, lambda E: getattr(E, name)(*a, **kw), r, w)
                else:
                    _, q, out, in_, r, w, _g = o
                    self.dma(q, out, in_, r, w)
                if g0 is None or idx[k] >= len(tasks[k]) or tasks[k][idx[k]][-1] != g0:
                    break

    def op(self, e, fn, reads=(), writes=()):
        if self.rec is not None:
            cap = _Cap()
            fn(cap)
            self.rec.append(("op", e, cap.call, tuple(reads), tuple(writes), self.gid))
            return None
        att = self._waits(e, reads, writes, attach=True)
        ins = fn(self.eng[e])
        if att is not None:
            ins._wait_ge(att[0], att[1])
        if self.cnt[e] >= self.SEM_ROLL:
            self._newsem(e)
        self.cnt[e] += 1
        ins.then_inc(self.sem[e], 1)
        self.n_inst[e] += 1
        tok = (self.sem[e], self.cnt[e], e)
        self.last[e] = tok
        self._done(tok, reads, writes)
        return tok

    def dma(self, q, out, in_, reads=(), writes=()):
        if self.rec is not None:
            self.rec.append(("dma", q, out, in_, tuple(reads), tuple(writes), None))
            return None
        kind = "sw" if q == "pool" else "hw"
        base, n = self.dma_rng[kind]
        slot = base + self.dma_i[kind] % n
        self.dma_i[kind] += 1
        sem = self.dma_sems[slot]
        pv = self.dma_val[slot]
        att = self._waits(q, reads, writes, attach=True)
        seen = self.seen[q]
        if pv > 0 and seen.get(id(sem), 0) < pv:
            self.eng[q].wait_ge(sem, pv)
            seen[id(sem)] = pv
        ins = self.eng[q].dma_start(out=out, in_=in_)
        if att is not None:
            ins._wait_ge(att[0], att[1])
        ins.then_inc(sem, 16)
        self.n_inst[q] += 1
        self.dma_val[slot] = pv + 16
        tok = (sem, pv + 16, "dma")
        self._done(tok, reads, writes)
        return tok

    def barrier(self):
        assert self.rec is None
        self._barrier()

    def _barrier(self):
        toks = list(self.last.values())
        for i, sem in enumerate(self.dma_sems):
            if self.dma_val[i] > 0:
                toks.append((sem, self.dma_val[i], "dma"))
        for e in self.eng:
            seen = self.seen[e]
            for (sem, val, src) in toks:
                if seen.get(id(sem), 0) < val:
                    self.eng[e].wait_ge(sem, val)
                    seen[id(sem)] = val

    def finish(self):
        self.barrier()
        self.es.close()


class Ring:
    def __init__(self, tiles):
        self.tiles = tiles
        self.bufs = [Buf() for _ in tiles]
        self.i = 0

    def next(self):
        k = self.i % len(self.tiles)
        self.i += 1
        return self.tiles[k], self.bufs[k]


def build(stage=99):
    no_sample = stage >= 10 and stage < 90
    if stage < 90:
        stage = stage % 10
    nc = bass.Bass("TRN2", target_bir_lowering=False)
    m = MK(nc)
    m.es.enter_context(nc.allow_low_precision(reason="bf16 matmul operands, fp32 accumulation"))

    def din(name, shape, dt=F32):
        return nc.dram_tensor(name, list(shape), dt, kind="ExternalInput").ap()

    def dout(name, shape, dt=F32):
        return nc.dram_tensor(name, list(shape), dt, kind="ExternalOutput").ap()

    def dscr(name, shape, dt=F32):
        return nc.dram_tensor(name, list(shape), dt, kind="Internal").ap()

    xin = din("xin", [NCH, 128, D])
    w0 = din("w0", [128, 8, NW0])
    w0o = din("w0o", [128, 8, D])
    w1 = din("w1", [128, 8, 4096])
    w1o = din("w1o", [128, 8, D])
    wmod = din("wmod", [2, 128, 8, 3072])
    bmod = din("bmod", [2, 3072])
    gpre = din("gpre", [2, D])
    gpost = din("gpost", [2, D])
    cvT = din("cvT", [128, 8, 2])
    bgate = din("bgate", [16])
    ghn = din("ghn", [512])
    gqk = din("gqk", [2, 64])
    ropec = din("ropec", [NOUT + NIN, 128, 64])
    ropes = din("ropes", [NOUT + NIN, 128, 64])
    ct0 = din("ct0", [8, 128, 129])
    m0 = din("m0", [8])
    flg = din("flg", [2, NOUT * 8])
    gkc = din("gkc", [128, 256])
    gvc = din("gvc", [2, 128, 2, 65])
    nkc = din("nkc", [128, 8, 256])
    nvc = din("nvc", [2, 128, 16, 65])
    ebias = din("ebias", [128, 16, 14, 64])
    cident = din("cident", [128, 128])
    cut = din("cut", [128, 128])
    clt = din("clt", [128, 128])
    cmut = din("cmut", [128, 128])
    cmlt = din("cmlt", [128, 128])
    csel = din("csel", [8, 8 * 128])
    cones = din("cones", [128, 128])
    cpick = din("cpick", [2, 128, 128])

    o_yp = dout("o_yp", [NPC, 128, D])
    o_ys = dout("o_ys", [NIN, 128, D])
    o_C = dout("o_C", [2, 8, 128, 128])
    o_n = dout("o_n", [2, 8, 128])
    o_m = dout("o_m", [2, 8])
    o_gk = dout("o_gk", [2, 2, 256, 64])
    o_gv = dout("o_gv", [2, 2, 256, 64])
    o_nk = dout("o_nk", [2, 16, 256, 64])
    o_nv = dout("o_nv", [2, 16, 256, 64])

    s_mod = dscr("s_mod", [2, 2, 3, D])
    s_x1 = dscr("s_x1", [NPC + NIN, 128, D])
    s_w1 = dscr("s_w1", [128, 8 * 4096], BF16)
    s_w1o = dscr("s_w1o", [128, 8 * D], BF16)
    w1f = w1.rearrange("p k c -> p (k c)")
    w1of = w1o.rearrange("p k c -> p (k c)")
    CV = 1024
    conv_chunks = [(w1f, s_w1, i * CV) for i in range(8 * 4096 // CV)] + [(w1of, s_w1o, i * CV) for i in range(8 * D // CV)]
    conv_state = {"in": 0, "out": 0, "pend": []}
    bWscs = []

    def conv_in(ring_):
        k = conv_state["in"]
        if k >= len(conv_chunks):
            return
        src, dst, off = conv_chunks[k]
        stg, bstg = ring_.next()
        m.dma("pool", stg[:], src[:, off:off + CV], writes=[bstg])
        conv_state["pend"].append((stg, bstg, dst, off))
        conv_state["in"] += 1

    def conv_out():
        if conv_state["pend"]:
            stg, bstg, dst, off = conv_state["pend"].pop(0)
            bc = Buf()
            bWscs.append(bc)
            m.dma("sp", dst[:, off:off + CV], stg[:], reads=[bstg], writes=[bc])
            conv_state["out"] += 1

    top = m.es

    uid = [0]

    def sb(es, name, shape, dt):
        uid[0] += 1
        return es.enter_context(nc.sbuf_tensor("%s_%d" % (name, uid[0]), list(shape), dt))

    pst = [top.enter_context(nc.psum_tensor("psd%d" % i, [128, 1024], F32)) for i in range(4)]
    psb = [Buf(excl=True) for _ in range(8)]
    pctr = [0, 0]

    pools = {"all": [0, 1, 2, 3, 4, 5], "a": [0, 1], "b": [2, 3], "c": [4, 5], "d": [6, 7]}
    pcur = ["all"]
    pcnt = {k: [0, 0] for k in pools}

    def ps1():
        banks = pools[pcur[0]]
        k = banks[pcnt[pcur[0]][0] % len(banks)]
        pcnt[pcur[0]][0] += 1
        return pst[k // 2][:, (k % 2) * 512:(k % 2) * 512 + 512], psb[k]

    def ps2():
        banks = pools[pcur[0]]
        nd = len(banks) // 2
        k = banks[0] // 2 + pcnt[pcur[0]][1] % nd
        pcnt[pcur[0]][1] += 1
        return pst[k], [psb[2 * k], psb[2 * k + 1]]

    def rec_task(pool, f, *a):
        old = pcur[0]
        pcur[0] = pool
        ops = m.record(f, *a)
        pcur[0] = old
        return ops

    identb = sb(top, "identb", [128, 128], BF16)
    identf = sb(top, "identf", [128, 128], F32)
    utf = sb(top, "utf", [128, 128], F32)
    ltf = sb(top, "ltf", [128, 128], F32)
    mut = sb(top, "mut", [128, 128], BF16)
    mlt = sb(top, "mlt", [128, 128], BF16)
    sel = sb(top, "sel", [8, 8 * 128], F32)
    onesf = sb(top, "onesf", [128, 128], F32)
    pick = sb(top, "pick", [128, 2, 128], F32)
    bgb = sb(top, "bgb", [128, 16], F32)
    ghnb = sb(top, "ghnb", [128, 512], F32)
    gqkb = sb(top, "gqkb", [128, 2, 64], F32)
    bK = Buf()
    m.dma("pool", identb[:], cident, writes=[bK])
    m.dma("sp", identf[:], cident, writes=[bK])
    m.dma("sp", utf[:], cut, writes=[bK])
    m.dma("sp", ltf[:], clt, writes=[bK])
    m.dma("pool", mut[:], cmut, writes=[bK])
    m.dma("pool", mlt[:], cmlt, writes=[bK])
    m.dma("sp", sel[:], csel, writes=[bK])
    m.dma("sp", onesf[:], cones, writes=[bK])
    m.dma("sp", pick[:], cpick.rearrange("a p c -> p a c"), writes=[bK])
    m.dma("sp", bgb[:], bgate.partition_broadcast(128), writes=[bK])
    m.dma("sp", ghnb[:], ghn.partition_broadcast(128), writes=[bK])
    m.dma("sp", gqkb[:], gqk.partition_broadcast(128), writes=[bK])

    def selj(j):
        return sel[:, j * 128:(j + 1) * 128]

    def act(fn, reads, writes):
        return m.op("act", fn, reads, writes)

    def dve(fn, reads, writes):
        return m.op("dve", fn, reads, writes)

    def pe(fn, reads, writes):
        return m.op("pe", fn, reads, writes)

    def evac(out_ap, in_ap, reads, writes, scale=None):
        if T.bph:
            if scale is None:
                dve(lambda E: E.tensor_copy(out=out_ap, in_=in_ap), reads, writes)
            else:
                dve(lambda E: E.tensor_scalar(out=out_ap, in0=in_ap, scalar1=scale, scalar2=None, op0=ALU.mult), reads, writes)
        else:
            if scale is None:
                act(lambda E: E.activation(out=out_ap, in_=in_ap, func=AF.Copy), reads, writes)
            else:
                act(lambda E: E.activation(out=out_ap, in_=in_ap, func=AF.Copy, scale=scale), reads, writes)

    def silu_from_psum(zbank, zbb, dst, bdst):
        tmp, btmp = T.tmpr.next()
        act(lambda E: E.activation(out=tmp[:], in_=zbank[:], func=AF.Exp, scale=-1.0), [zbb], [btmp])
        dve(lambda E: E.tensor_scalar(out=tmp[:], in0=tmp[:], scalar1=1.0, scalar2=None, op0=ALU.add), [btmp], [btmp])
        dve(lambda E: E.reciprocal(out=tmp[:], in_=tmp[:]), [btmp], [btmp])
        dve(lambda E: E.tensor_tensor(out=dst, in0=zbank[:], in1=tmp[:], op=ALU.mult), [zbb, btmp], [bdst])

    def rstd_from_ssq(st, bst, n, inv_n):
        dve(lambda E: E.tensor_scalar(out=st[:, 0:n], in0=st[:, 0:n], scalar1=inv_n, scalar2=EPS,
                                      op0=ALU.mult, op1=ALU.add), [bst], [bst])
        act(lambda E: E.activation(out=st[:, 0:n], in_=st[:, 0:n], func=AF.Ln), [bst], [bst])
        act(lambda E: E.activation(out=st[:, 0:n], in_=st[:, 0:n], func=AF.Exp, scale=-0.5), [bst], [bst])

    def ring(es, name, shape, dt, n):
        return Ring([sb(es, "%s%d" % (name, i), shape, dt) for i in range(n)])

    class NS:
        pass

    T = NS()
    T.bph = False
    class _StRings:
        def __init__(self):
            self.r = {k: ring(top, "st" + k, [128, 16], F32, 6) for k in pools}

        def next(self):
            return self.r[pcur[0]].next()

    str_ = _StRings()
    LW = ExitStack()
    W0 = sb(LW, "W0", [128, 8, NW0], BF16)
    W0o = sb(LW, "W0o", [128, 8, D], BF16)
    bW0k = [Buf() for _ in range(8)]
    bW0ok = [Buf() for _ in range(8)]

    with ExitStack() as p0:
        cv = sb(p0, "cv", [128, 8, 2], F32)
        cvb = sb(p0, "cvb", [128, 8, 2], BF16)
        bcv = Buf()
        m.dma("sp", cv[:], cvT, writes=[bcv])
        act(lambda E: E.activation(out=cvb[:], in_=cv[:], func=AF.Silu), [bcv], [bcv])
        wmt = [sb(p0, "wmt%d" % i, [128, 8, 512], BF16) for i in range(2)]
        wmr = Ring(wmt)
        mrow = sb(p0, "mrow", [2, 3072], F32)
        brow = sb(p0, "brow", [2, 3072], F32)
        gpr = sb(p0, "gpr", [2, D], F32)
        gpo = sb(p0, "gpo", [2, D], F32)
        drow = sb(p0, "drow", [2, 3, D], F32)
        bmr = Buf()
        bbr = Buf()
        bdr = Buf()
        for L in range(2):
            m.dma("sp", brow[:], bmod[L].partition_broadcast(2), writes=[bbr])
            m.dma("sp", gpr[:], gpre[L].partition_broadcast(2), writes=[bbr])
            m.dma("sp", gpo[:], gpost[L].partition_broadcast(2), writes=[bbr])
            for cb in range(6):
                wt, bw = wmr.next()
                m.dma("pool", wt[:], wmod[L][:, :, cb * 512:(cb + 1) * 512], writes=[bw])
                bank, bb = ps1()
                with m.group():
                    for kt in range(8):
                        pe(lambda E: E.matmul(bank[0:2, :], cvb[:, kt, :], wt[:, kt, :], start=(kt == 0), stop=(kt == 7)),
                           [bcv, bw], [bb])
                dve(lambda E: E.tensor_tensor(out=mrow[:, cb * 512:(cb + 1) * 512], in0=bank[0:2, :],
                                              in1=brow[:, cb * 512:(cb + 1) * 512], op=ALU.add), [bb, bbr], [bmr])
            dve(lambda E: E.scalar_tensor_tensor(out=drow[:, 0, :], in0=mrow[:, D:2 * D], scalar=1.0, in1=gpr[:],
                                                 op0=ALU.add, op1=ALU.mult), [bmr, bbr], [bdr])
            dve(lambda E: E.tensor_copy(out=drow[:, 1, :], in_=mrow[:, 0:D]), [bmr], [bdr])
            dve(lambda E: E.tensor_tensor(out=drow[:, 2, :], in0=mrow[:, 2 * D:3 * D], in1=gpo[:], op=ALU.mult),
                [bmr, bbr], [bdr])
            m.dma("sp", s_mod[L], drow[:], reads=[bdr], writes=[bK])
        for kt in range(8):
            m.dma("pool", W0[:, kt, :], w0[:, kt, :], writes=[bW0k[kt]])
        for kt in range(0, 8, 4):
            m.dma("pool", W0o[:, kt:kt + 4, :], w0o[:, kt:kt + 4, :], writes=bW0ok[kt:kt + 4])
        m.barrier()

    L0 = ExitStack()
    NKT = 34
    KTg = sb(L0, "KTg", [128, NKT * 128], BF16)
    Vg = sb(L0, "Vg", [128, NKT, 2, 65], BF16)
    bKT = [Buf() for _ in range(NKT)]
    bVg = [Buf() for _ in range(NKT)]
    dve(lambda E: E.memset(Vg[:], 1.0), [], bVg)
    hTs = sb(L0, "hTs", [128, NIN, 8, 128], BF16)
    bhT = [Buf() for _ in range(NIN)]
    hfs = sb(L0, "hfs", [128, NIN, 512], BF16)
    bhf = [Buf() for _ in range(NIN)]
    CT = sb(L0, "CT", [128, 8, 129], F32)
    CTb = sb(L0, "CTb", [128, 8, 129], BF16)
    bCT = [Buf() for _ in range(8)]
    bCTb = [Buf() for _ in range(8)]
    mring = [Ring([sb(L0, "mbc%d_%d" % (d, i), [128, 4], F32) for i in range(3)]) for d in range(2)]
    flgt = sb(L0, "flgt", [128, 2, NOUT * 8], F32)
    bflg = Buf()
    m.dma("sp", flgt[:], flg.partition_broadcast(128), writes=[bflg])
    dgr = ring(L0, "dgr", [8, 8], F32, 2)
    smallr = ring(L0, "smallr", [128, 128], F32, 1)

    def alloc_mlstm(es, deep=False):
        nd = 3 if deep else 2
        T.ktokr = ring(es, "ktok", [128, 512], BF16, nd)
        T.kTr = ring(es, "kT", [128, 512], BF16, nd)
        T.qtokr = ring(es, "qtok", [128, 512], BF16, 1)
        T.qTr = ring(es, "qT", [128, 512], BF16, nd)
        T.vaugr = ring(es, "vaug", [128, 4, 129], BF16, nd)
        for t in T.vaugr.tiles:
            dve(lambda E: E.memset(t[:], 1.0), [], T.vaugr.bufs)
        T.gtr = ring(es, "gtr", [128, 96], F32, 3)
        T.rowr = ring(es, "rowr", [8, 256], F32, 3)
        T.urr = ring(es, "urr", [8, 256], F32, 2)
        T.DTr = ring(es, "DTr", [128, 512], BF16, 1)
        T.PTr = ring(es, "PTr", [128, 512], BF16, 1)
        T.hnr = ring(es, "hnr", [128, 2, 129], F32, 1)
        T.wkvr = ring(es, "wkvr", [128, 129], BF16, 2)
        T.sqr = ring(es, "sqr", [128, 512], F32, 1)
        T.q8r = ring(es, "q8r", [128, 512], F32, 2)
        T.q8br = ring(es, "q8br", [128, 512], BF16, 1)
        T.rtr = ring(es, "rtr", [128, 2, 64], F32, 2)

    def alloc_F(es):
        alloc_mlstm(es, deep=True)
        T.bph = False
        T.modg = sb(es, "modg", [128, D], F32)
        T.mods = sb(es, "mods", [128, D], F32)
        T.bmod = Buf()
        T.xr = ring(es, "xr", [128, D], F32, 2)
        T.jr = ring(es, "jr", [128, D], BF16, 1)
        T.hr = ring(es, "hr", [128, D], BF16, 1)
        T.kvr = ring(es, "kvr", [128, 272], F32, 2)
        T.cvr = ring(es, "cvr", [128, CV], BF16, 2)

    def alloc_B(es):
        alloc_mlstm(es)
        T.bph = True
        T.modgg = sb(es, "modgg", [128, D], F32)
        T.bmod = Buf()
        T.hbr = ring(es, "hbr", [128, 512], F32, 1)
        T.QTr = ring(es, "QTr", [128, 2, 512], BF16, 2)
        for t_ in T.QTr.tiles:
            dve(lambda E: E.memset(t_[64:128, 0, :], 0.0), [], T.QTr.bufs)
            dve(lambda E: E.memset(t_[0:64, 1, :], 0.0), [], T.QTr.bufs)
        T.soar = ring(es, "soar", [128, 512], BF16, 2)
        T.szr = ring(es, "szr", [128, D], BF16, 1)
        T.ypr = ring(es, "ypr", [128, D], BF16, 2)
        T.tmpr = ring(es, "tmpr", [128, 512], F32, 1)
        T.sqCr = ring(es, "sqCr", [128, 512], F32, 1)
        T.sgr = T.q8r
        T.ybr = ring(es, "ybr", [128, D], BF16, 1)
        T.yTr = ring(es, "yTr", [128, 8, 128], BF16, 1)
        T.PGr = ring(es, "PGr", [128, 512], BF16, 3)
        T.outr = ring(es, "outr", [128, 512], F32, 1)

    def prenorm(src, hT_dst, b_dst, src_reads=()):
        xt, bx = T.xr.next()
        m.dma("sp", xt[:], src, reads=list(src_reads), writes=[bx])
        st, bst = str_.next()
        jt, bj = T.jr.next()
        dve(lambda E: E.memset(st[:, 0:1], 0.0), [], [bst])
        act(lambda E: E.activation(out=jt[:], in_=xt[:], func=AF.Square, accum_out=st[:, 0:1]), [bx, bst], [bj, bst])
        rstd_from_ssq(st, bst, 1, 1.0 / D)
        ht, bh = T.hr.next()
        dve(lambda E: E.scalar_tensor_tensor(out=xt[:], in0=xt[:], scalar=st[:, 0:1], in1=T.modg[:],
                                             op0=ALU.mult, op1=ALU.mult), [bx, bst, T.bmod], [bx])
        m.op("pool", lambda E: E.tensor_tensor(out=ht[:], in0=xt[:], in1=T.mods[:], op=ALU.add), [bx, T.bmod], [bh])
        pp, pb = ps2()
        with m.group():
            for kt in range(8):
                pe(lambda E: E.matmul(pp[:, kt * 128:(kt + 1) * 128], ht[:, kt * 128:(kt + 1) * 128], identb[:],
                                      start=True, stop=True), [bh, bK], pb)
        act(lambda E: E.activation(out=hT_dst[:, 0:4, :], in_=pp[:, 0:512].rearrange("p (a b) -> p a b", a=4),
                                   func=AF.Copy), pb, [b_dst])
        dve(lambda E: E.tensor_copy(out=hT_dst[:, 4:8, :], in_=pp[:, 512:1024].rearrange("p (a b) -> p a b", a=4)),
            pb, [b_dst])

    def proj(hT, bh, c0, n):
        bank, bb = ps1()
        with m.group():
            for kt in range(8):
                pe(lambda E: E.matmul(bank[:, 0:n], hT[:, kt, :], W0[:, kt, c0:c0 + n], start=(kt == 0), stop=(kt == 7)),
                   [bh, bW0k[kt]], [bb])
        return bank, bb

    def transpose4(src, bsrc, dst, bdst, eng="act"):
        bank, bb = ps1()
        for h in range(4):
            pe(lambda E: E.matmul(bank[:, h * 128:(h + 1) * 128], src[:, h * 128:(h + 1) * 128], identb[:],
                                  start=True, stop=True), [bsrc, bK], [bb])
        if eng == "act":
            act(lambda E: E.activation(out=dst[:], in_=bank[:], func=AF.Copy), [bb], [bdst])
        else:
            dve(lambda E: E.tensor_copy(out=dst[:], in_=bank[:]), [bb], [bdst])

    def gates_prep(kvbank, bkv, slot=None):
        gt, bg = T.gtr.next()
        rw, brw = T.rowr.next()
        dg, bdg = dgr.next()
        gsrc = kvbank[:, 256:272]
        dve(lambda E: E.memset(gt[:], 0.0), [], [bg])
        dve(lambda E: E.tensor_tensor(out=gt[:, 0:16], in0=gsrc, in1=bgb[:], op=ALU.add), [bkv, bK], [bg])
        act(lambda E: E.activation(out=gt[:, 8:16], in_=gt[:, 8:16], func=AF.Exp, scale=-1.0), [bg], [bg])
        act(lambda E: E.activation(out=gt[:, 8:16], in_=gt[:, 8:16], func=AF.Ln, bias=1.0), [bg], [bg])
        dve(lambda E: E.tensor_scalar(out=gt[:, 8:16], in0=gt[:, 8:16], scalar1=-1.0, scalar2=None, op0=ALU.mult),
            [bg], [bg])
        if slot is not None:
            f = flgt[:, 0, slot * 8:(slot + 1) * 8]
            nf = flgt[:, 1, slot * 8:(slot + 1) * 8]
            dve(lambda E: E.tensor_tensor(out=gt[:, 8:16], in0=gt[:, 8:16], in1=f, op=ALU.mult), [bg, bflg], [bg])
            dve(lambda E: E.tensor_tensor(out=gt[:, 0:8], in0=gt[:, 0:8], in1=f, op=ALU.mult), [bg, bflg], [bg])
            dve(lambda E: E.tensor_tensor(out=gt[:, 0:8], in0=gt[:, 0:8], in1=nf, op=ALU.add), [bg, bflg], [bg])
        bank, bb = ps1()
        pe(lambda E: E.matmul(bank[:, 0:4], utf[:], gt[:, 8:12], start=True, stop=True), [bg, bK], [bb])
        pe(lambda E: E.matmul(bank[:, 4:8], ltf[:], gt[:, 12:16], start=True, stop=True), [bg, bK], [bb])
        pe(lambda E: E.matmul(bank[:, 8:16], onesf[:], gt[:, 8:16], start=True, stop=True), [bg, bK], [bb])
        act(lambda E: E.activation(out=gt[:, 16:32], in_=bank[:, 0:16], func=AF.Copy), [bb], [bg])
        dve(lambda E: E.tensor_tensor(out=gt[:, 32:40], in0=gt[:, 0:8], in1=gt[:, 16:24], op=ALU.subtract), [bg], [bg])
        pe(lambda E: E.matmul(bank[0:8, 128:256], gt[:, 32:40], identf[:], start=True, stop=True), [bg, bK], [bb])
        act(lambda E: E.activation(out=rw[:, 0:128], in_=bank[0:8, 128:256], func=AF.Copy), [bb], [brw])
        dve(lambda E: E.tensor_reduce(out=dg[:, 0:1], in_=rw[:, 0:128], axis=AX.X, op=ALU.max), [brw], [bdg])
        dve(lambda E: E.tensor_scalar(out=dg[:, 0:8], in0=identf[0:8, 0:8], scalar1=dg[:, 0:1], scalar2=None,
                                      op0=ALU.mult), [bdg, bK], [bdg])
        pe(lambda E: E.matmul(bank[:, 256:264], onesf[0:8, :], dg[:, 0:8], start=True, stop=True), [bdg, bK], [bb])
        act(lambda E: E.activation(out=gt[:, 40:48], in_=bank[:, 256:264], func=AF.Copy), [bb], [bg])
        return gt, bg, rw, brw

    def seq_scalar_step(d, gt, bg, mstate):
        mo, bmo = mstate[d]
        mn, bmn = mring[d].next()
        o = d * 4
        dve(lambda E: E.tensor_tensor(out=gt[:, 56 + o:60 + o], in0=mo[:], in1=gt[:, 40 + o:44 + o], op=ALU.max),
            [bmo, bg], [bg])
        dve(lambda E: E.tensor_tensor(out=mn[:], in0=gt[:, 24 + o:28 + o], in1=gt[:, 56 + o:60 + o], op=ALU.add),
            [bg], [bmn])
        dve(lambda E: E.tensor_tensor(out=gt[:, 48 + o:52 + o], in0=gt[:, 32 + o:36 + o], in1=gt[:, 56 + o:60 + o],
                                      op=ALU.subtract), [bg], [bg])
        dve(lambda E: E.tensor_tensor(out=gt[:, 56 + o:60 + o], in0=mo[:], in1=gt[:, 56 + o:60 + o], op=ALU.subtract),
            [bmo, bg], [bg])
        act(lambda E: E.activation(out=gt[:, 48 + o:52 + o], in_=gt[:, 48 + o:52 + o], func=AF.Exp), [bg], [bg])
        act(lambda E: E.activation(out=gt[:, 56 + o:60 + o], in_=gt[:, 56 + o:60 + o], func=AF.Exp), [bg], [bg])
        mstate[d] = (mn, bmn)
        return mo, bmo

    def state_update(d, h, gt, bg, ktok, bkt, vaug, bva):
        j = d * 4 + h
        wt, bwt = T.wkvr.next()
        evac(wt[:], vaug[:, h, :], [bva, bg], [bwt], scale=gt[:, 48 + j:49 + j])
        bank, bb = ps1()
        pe(lambda E: E.matmul(bank[:, 0:129], ktok[:, h * 128:(h + 1) * 128], wt[:], start=True, stop=True),
           [bkt, bwt], [bb])
        dve(lambda E: E.scalar_tensor_tensor(out=CT[:, j, :], in0=CT[:, j, :], scalar=gt[:, 56 + j:57 + j],
                                             in1=bank[:, 0:129], op0=ALU.mult, op1=ALU.add), [bCT[j], bg, bb], [bCT[j]])

    def full_step(d, gt, bg, rw, brw, mo, bmo, qT, bqT, kT, bkT_, ktok, bkt, vaug, bva, hdst, bhd):
        o = d * 4
        mC = mlt if d == 0 else mut
        mA = mut if d == 0 else mlt
        cbank, cb = ps1()
        for h in range(4):
            j = o + h
            pe(lambda E: E.matmul(cbank[:, h * 128:(h + 1) * 128], selj(j), rw[:, 0:128], start=True, stop=False),
               [bK, brw], [cb])
            pe(lambda E: E.matmul(cbank[:, h * 128:(h + 1) * 128], identb[:], mC[:], start=False, stop=True), [bK], [cb])
        dve(lambda E: E.tensor_reduce(out=gt[:, 64 + o:68 + o], in_=cbank.rearrange("p (a b) -> p a b", a=4),
                                      axis=AX.X, op=ALU.max), [cb], [bg])
        dve(lambda E: E.tensor_tensor(out=gt[:, 72 + o:76 + o], in0=mo[:], in1=gt[:, 64 + o:68 + o], op=ALU.max),
            [bmo, bg], [bg])
        dve(lambda E: E.tensor_scalar(out=gt[:, 72 + o:76 + o], in0=gt[:, 72 + o:76 + o], scalar1=-1.0, scalar2=None,
                                      op0=ALU.mult), [bg], [bg])
        dve(lambda E: E.tensor_tensor(out=gt[:, 80 + o:84 + o], in0=mo[:], in1=gt[:, 72 + o:76 + o], op=ALU.add),
            [bmo, bg], [bg])
        dve(lambda E: E.tensor_tensor(out=gt[:, 88 + o:92 + o], in0=gt[:, 72 + o:76 + o], in1=gt[:, 16 + o:20 + o],
                                      op=ALU.subtract), [bg], [bg])
        act(lambda E: E.activation(out=gt[:, 80 + o:84 + o], in_=gt[:, 80 + o:84 + o], func=AF.Exp), [bg], [bg])
        act(lambda E: E.activation(out=gt[:, 88 + o:92 + o], in_=gt[:, 88 + o:92 + o], func=AF.Exp), [bg], [bg])
        ub, ubb = ps1()
        pe(lambda E: E.matmul(ub[0:8, 0:128], gt[:, 72:80], identf[:], start=True, stop=True), [bg, bK], [ubb])
        ur, bur = T.urr.next()
        evac(ur[:, 128:256], ub[0:8, 0:128], [ubb], [bur])
        abank, ab = ps1()
        for h in range(4):
            j = o + h
            pe(lambda E: E.matmul(abank[:, h * 128:(h + 1) * 128], rw[:, 0:128], selj(j), start=True, stop=False),
               [brw, bK], [ab])
            pe(lambda E: E.matmul(abank[:, h * 128:(h + 1) * 128], selj(j), ur[:, 128:256], start=False, stop=False),
               [bur, bK], [ab])
            pe(lambda E: E.matmul(abank[:, h * 128:(h + 1) * 128], identb[:], mA[:], start=False, stop=True), [bK], [ab])
        DT, bDT = T.DTr.next()
        act(lambda E: E.activation(out=DT[:], in_=abank[:], func=AF.Exp), [ab], [bDT])
        sbank, sbb = ps1()
        for h in range(4):
            pe(lambda E: E.matmul(sbank[:, h * 128:(h + 1) * 128], kT[:, h * 128:(h + 1) * 128],
                                  qT[:, h * 128:(h + 1) * 128], start=True, stop=True), [bkT_, bqT], [sbb])
        PT, bPT = T.PTr.next()
        dve(lambda E: E.tensor_tensor(out=PT[:], in0=DT[:], in1=sbank[:], op=ALU.mult), [bDT, sbb], [bPT])
        for hp in range(2):
            ib, ibb = ps1()
            nb, nbb = ps1()
            for hh in range(2):
                h = hp * 2 + hh
                j = o + h
                m.op("pool", lambda E: E.tensor_copy(out=CTb[:, j, :], in_=CT[:, j, :]), [bCT[j]], [bCTb[j]])
                pe(lambda E: E.matmul(ib[:, hh * 129:(hh + 1) * 129], qT[:, h * 128:(h + 1) * 128], CTb[:, j, :],
                                      start=True, stop=True), [bqT, bCTb[j]], [ibb])
                pe(lambda E: E.matmul(nb[:, hh * 129:(hh + 1) * 129], PT[:, h * 128:(h + 1) * 128], vaug[:, h, :],
                                      start=True, stop=True), [bPT, bva], [nbb])
            hn, bhn = T.hnr.next()
            for hh in range(2):
                h = hp * 2 + hh
                j = o + h
                evac(hn[:, hh, :], ib[:, hh * 129:(hh + 1) * 129], [ibb, bg], [bhn], scale=gt[:, 80 + j:81 + j])
            dve(lambda E: E.tensor_tensor(out=hn[:], in0=hn[:], in1=nb[:, 0:258].rearrange("p (a b) -> p a b", a=2),
                                          op=ALU.add), [bhn, nbb], [bhn])
            st, bst = str_.next()
            dve(lambda E: E.scalar_tensor_tensor(out=st[:, 0:2], in0=hn[:, :, 128], scalar=-1.0, in1=hn[:, :, 128],
                                                 op0=ALU.mult, op1=ALU.max), [bhn], [bst])
            dve(lambda E: E.tensor_tensor(out=st[:, 0:2], in0=st[:, 0:2], in1=gt[:, 88 + o + hp * 2:90 + o + hp * 2],
                                          op=ALU.max), [bst, bg], [bst])
            dve(lambda E: E.reciprocal(out=st[:, 0:2], in_=st[:, 0:2]), [bst], [bst])
            dve(lambda E: E.tensor_tensor(out=hdst[:, hp * 256:(hp + 1) * 256].rearrange("p (a b) -> p a b", a=2),
                                          in0=hn[:, :, 0:128], in1=st[:, 0:2].unsqueeze(2).to_broadcast([128, 2, 128]),
                                          op=ALU.mult), [bhn, bst], [bhd])
        for h in range(4):
            state_update(d, h, gt, bg, ktok, bkt, vaug, bva)

    def rope(src, bsrc, nh, rope_idx, dst, bdst):
        rt, brt = T.rtr.next()
        m.dma("sp", rt[:, 0, :], ropec[rope_idx], writes=[brt])
        m.dma("sp", rt[:, 1, :], ropes[rope_idx], writes=[brt])
        sq, bsq = T.sqr.next()
        n = nh * 64
        x3 = src[:, 0:n].rearrange("p (h e) -> p h e", h=nh)
        dve(lambda E: E.tensor_tensor(out=sq[:, 0:n].rearrange("p (h e) -> p h e", h=nh), in0=x3,
                                      in1=rt[:, 0, :].unsqueeze(1).to_broadcast([128, nh, 64]), op=ALU.mult),
            [bsrc, brt], [bsq])
        x5 = src[:, 0:n].rearrange("p (h r f e) -> p h r f e", h=nh, r=2, f=2)
        tmp, btmp = T.q8r.next()
        o5 = tmp[:, 0:n].rearrange("p (h r f e) -> p h r f e", h=nh, r=2, f=2)
        s4 = rt[:, 1, :].rearrange("p (r f e) -> p r f e", r=2, f=2)
        for f in range(2):
            m.op("pool", lambda E: E.tensor_tensor(out=o5[:, :, :, f, :], in0=x5[:, :, :, 1 - f, :],
                                                   in1=s4[:, :, f, :].unsqueeze(1).to_broadcast([128, nh, 2, 16]),
                                                   op=ALU.mult), [bsrc, brt], [btmp])
        dve(lambda E: E.tensor_tensor(out=dst[:, 0:n], in0=sq[:, 0:n], in1=tmp[:, 0:n], op=ALU.add), [bsq, btmp], [bdst])

    def headnorm(x, bx, nh, hd, gain_ap, sq_ring=None):
        sq, bsq = (sq_ring or T.sqr).next()
        st, bst = str_.next()
        n = nh * hd
        dve(lambda E: E.tensor_tensor(out=sq[:, 0:n], in0=x[:, 0:n], in1=x[:, 0:n], op=ALU.mult), [bx], [bsq])
        dve(lambda E: E.tensor_reduce(out=st[:, 0:nh], in_=sq[:, 0:n].rearrange("p (a b) -> p a b", a=nh), axis=AX.X,
                                      op=ALU.add), [bsq], [bst])
        rstd_from_ssq(st, bst, nh, 1.0 / hd)
        x3 = x[:, 0:n].rearrange("p (a b) -> p a b", a=nh)
        dve(lambda E: E.tensor_tensor(out=x3, in0=x3, in1=st[:, 0:nh].unsqueeze(2).to_broadcast([128, nh, hd]),
                                      op=ALU.mult), [bx, bst], [bx])
        dve(lambda E: E.tensor_tensor(out=x3, in0=x3, in1=gain_ap, op=ALU.mult), [bx, bK], [bx])

    def kv_process(kvbank, bkv, tile_idx, rope_idx, out_seq=None, out_half=None):
        kv, bkvt = T.kvr.next()
        act(lambda E: E.activation(out=kv[:, 0:272], in_=kvbank[:, 0:272], func=AF.Copy), [bkv], [bkvt])
        headnorm(kv, bkvt, 2, 64, gqkb[:, 1, :].unsqueeze(1).to_broadcast([128, 2, 64]))
        if out_seq is not None:
            for g in range(2):
                m.dma("sp", o_gk[out_seq, g, out_half * 128:(out_half + 1) * 128, :], kv[:, g * 64:(g + 1) * 64],
                      reads=[bkvt])
                m.dma("sp", o_gv[out_seq, g, out_half * 128:(out_half + 1) * 128, :],
                      kv[:, 128 + g * 64:128 + (g + 1) * 64], reads=[bkvt])
        kb16, bk16 = T.q8br.next()
        if rope_idx is not None:
            rope(kv, bkvt, 2, rope_idx, kb16, bk16)
        else:
            dve(lambda E: E.tensor_copy(out=kb16[:, 0:128], in_=kv[:, 0:128]), [bkvt], [bk16])
        bank, bb = ps1()
        pe(lambda E: E.matmul(bank[:, 0:128], kb16[:, 0:128], identb[:], start=True, stop=True), [bk16, bK], [bb])
        act(lambda E: E.activation(out=KTg[:, tile_idx * 128:(tile_idx + 1) * 128], in_=bank[:, 0:128], func=AF.Copy),
            [bb], [bKT[tile_idx]])
        dve(lambda E: E.tensor_copy(out=Vg[:, tile_idx, :, 0:64], in_=kv[:, 128:256].rearrange("p (a b) -> p a b", a=2)),
            [bkvt], [bVg[tile_idx]])
        return kv, bkvt

    def mlstm_inputs(hT, bh, need_q):
        kbank, kbb = proj(hT, bh, C_KA, 512)
        ktok, bkt = T.ktokr.next()
        evac(ktok[:], kbank[:], [kbb], [bkt], scale=128.0 ** -0.5)
        vbank, vbb = proj(hT, bh, C_VA, 512)
        vaug, bva = T.vaugr.next()
        evac(vaug[:, :, 0:128], vbank.rearrange("p (a b) -> p a b", a=4), [vbb], [bva])
        if not need_q:
            return ktok, bkt, vaug, bva
        kT, bkT_ = T.kTr.next()
        transpose4(ktok, bkt, kT, bkT_, "dve")
        qbank, qbb = proj(hT, bh, C_QA, 512)
        qtok, bqt = T.qtokr.next()
        evac(qtok[:], qbank[:], [qbb], [bqt])
        qT, bqT = T.qTr.next()
        transpose4(qtok, bqt, qT, bqT, "dve" if T.bph else "act")
        return ktok, bkt, vaug, bva, kT, bkT_, qT, bqT

    def reset_state(d, mstate):
        dve(lambda E: E.memset(CT[:, d * 4:(d + 1) * 4, :], 0.0), [], bCT[d * 4:(d + 1) * 4])
        mt, bm = mring[d].next()
        dve(lambda E: E.memset(mt[:], 0.0), [], [bm])
        mstate[d] = (mt, bm)

    def write_state(seq_i, d, mstate):
        for h in range(4):
            j = d * 4 + h
            bank, bb = ps1()
            pe(lambda E: E.matmul(bank[:, 0:128], CT[:, j, 0:128], identf[:], start=True, stop=True), [bCT[j], bK], [bb])
            sm, bsm = smallr.next()
            act(lambda E: E.activation(out=sm[:], in_=bank[:, 0:128], func=AF.Copy), [bb], [bsm])
            m.dma("sp", o_C[seq_i, j], sm[:], reads=[bsm])
            m.dma("sp", o_n[seq_i, j].rearrange("(k o) -> k o", o=1), CT[:, j, 128:129], reads=[bCT[j]])
        mt, bm = mstate[d]
        m.dma("sp", o_m[seq_i, d * 4:(d + 1) * 4].rearrange("(o k) -> o k", o=1), mt[0:1, :], reads=[bm])

    def run_sequence(chunks, is_prompt, seq_i, n_out_slots, var, x1_base):
        n = len(chunks)
        nkt = n if is_prompt else NKT
        mstate = [None, None]
        fs = ExitStack()
        alloc_F(fs)
        m.dma("sp", T.modg[:], s_mod[0, var, 0].partition_broadcast(128), reads=[bK], writes=[T.bmod])
        m.dma("sp", T.mods[:], s_mod[0, var, 1].partition_broadcast(128), reads=[bK], writes=[T.bmod])
        if is_prompt:
            dve(lambda E: E.memset(CT[:], 0.0), [], bCT)
            for d in range(2):
                mt, bm = mring[d].next()
                dve(lambda E: E.memset(mt[:], 0.0), [], [bm])
                mstate[d] = (mt, bm)
        else:
            m.dma("sp", CT[:], ct0.rearrange("j k c -> k j c"), writes=bCT)
            for d in range(2):
                mt, bm = mring[d].next()
                m.dma("sp", mt[:], m0[d * 4:(d + 1) * 4].partition_broadcast(128), writes=[bm])
                mstate[d] = (mt, bm)
            m.dma("pool", KTg[:, 32 * 128:34 * 128], gkc, writes=[bKT[32], bKT[33]])
            m.dma("pool", Vg[:, 32:34, :, :], gvc.rearrange("t p g c -> p t g c"), writes=[bVg[32], bVg[33]])
        items = [("out", s_) for s_ in range(n_out_slots)] + [("fwd", i) for i in range(n)]
        H = {}

        def hT_of(it):
            kind, k = it
            return (hTs[:, k % 2], bhT[k % 2]) if kind == "out" else (hTs[:, k], bhT[k])

        def P1(it):
            kind, k = it
            hT, bh = hT_of(it)
            prenorm(xin[NPC + k] if kind == "out" else xin[chunks[k]], hT, bh)
            if not is_prompt:
                conv_out()
                conv_out()
                conv_in(T.cvr)
                conv_in(T.cvr)

        def P2(it):
            kind, k = it
            hT, bh = hT_of(it)
            h_ = {}
            if kind == "out":
                h_["mi"] = mlstm_inputs(hT, bh, False)
                kvbank, bkv = proj(hT, bh, C_KV, 272)
                h_["kv"] = kv_process(kvbank, bkv, k, k)
            else:
                h_["mi"] = mlstm_inputs(hT, bh, True)
                kvbank, bkv = proj(hT, bh, C_KV, 272)
                tile_idx = k if is_prompt else n_out_slots + k
                h_["kv"] = kv_process(kvbank, bkv, tile_idx, None if is_prompt else n_out_slots + k,
                                      out_seq=(seq_i + k // 2) if is_prompt else None, out_half=k % 2)
            H[it] = h_

        def P3(it):
            kind, k = it
            kvt, bkvt = H[it]["kv"]
            H[it]["g"] = gates_prep(kvt, bkvt, slot=k if kind == "out" else None)

        def Cst(it):
            kind, k = it
            h_ = H.pop(it)
            gt, bg, rw, brw = h_["g"]
            if kind == "out":
                ktok, bkt, vaug, bva = h_["mi"]
                for d in range(2):
                    seq_scalar_step(d, gt, bg, mstate)
                    for h in range(4):
                        state_update(d, h, gt, bg, ktok, bkt, vaug, bva)
            else:
                ktok, bkt, vaug, bva, kT, bkT_, qT, bqT = h_["mi"]
                if is_prompt and k % 2 == 0 and k > 0:
                    reset_state(0, mstate)
                mo, bmo = seq_scalar_step(0, gt, bg, mstate)
                full_step(0, gt, bg, rw, brw, mo, bmo, qT, bqT, kT, bkT_, ktok, bkt, vaug, bva, hfs[:, k, :], bhf[k])
                if is_prompt and k % 2 == 1:
                    write_state(seq_i + k // 2, 0, mstate)

        N = len(items)
        for t in range(N + 3):
            tasks = []
            if t < N:
                tasks.append(rec_task("a", P1, items[t]))
            if 0 <= t - 1 < N:
                tasks.append(rec_task("b", P2, items[t - 1]))
            if 0 <= t - 2 < N:
                tasks.append(rec_task("d", P3, items[t - 2]))
            if 0 <= t - 3 < N:
                tasks.append(rec_task("c", Cst, items[t - 3]))
            m.replay(tasks)
        if not is_prompt:
            while conv_state["pend"] or conv_state["in"] < len(conv_chunks):
                conv_out()
                conv_in(T.cvr)
        m.barrier()
        fs.close()
        if stage < 2:
            return
        bs_ = ExitStack()
        alloc_B(bs_)
        m.dma("sp", T.modgg[:], s_mod[0, var, 2].partition_broadcast(128), reads=[bK], writes=[T.bmod])
        order = [1, 0, 3, 2][:n] if is_prompt else list(range(n - 1, -1, -1))
        HB = {}

        def PB(i):
            hT, bh = hTs[:, i], bhT[i]
            h_ = {}
            h_["mi"] = mlstm_inputs(hT, bh, True)
            kvbank, bkv = proj(hT, bh, C_KV, 272)
            h_["g"] = gates_prep(kvbank, bkv)
            obank, obb = proj(hT, bh, C_OA, 512)
            soa, bsoa = T.soar.next()
            sgt, bsgt = T.sgr.next()
            act(lambda E: E.activation(out=sgt[:], in_=obank[:], func=AF.Exp, scale=-1.0), [obb], [bsgt])
            dve(lambda E: E.tensor_scalar(out=sgt[:], in0=sgt[:], scalar1=1.0, scalar2=None, op0=ALU.add), [bsgt], [bsgt])
            dve(lambda E: E.reciprocal(out=soa[:], in_=sgt[:]), [bsgt], [bsoa])
            h_["soa"] = (soa, bsoa)
            qbank, qbb = proj(hT, bh, C_QB, 512)
            q8, bq8 = T.q8r.next()
            evac(q8[:], qbank[:], [qbb], [bq8])
            headnorm(q8, bq8, 8, 64, gqkb[:, 0, :].unsqueeze(1).to_broadcast([128, 8, 64]))
            q16, bq16 = T.q8br.next()
            if is_prompt:
                dve(lambda E: E.tensor_copy(out=q16[:], in_=q8[:]), [bq8], [bq16])
            else:
                rope(q8, bq8, 8, n_out_slots + i, q16, bq16)
            tb, tbb = ps1()
            for p in range(4):
                pe(lambda E: E.matmul(tb[:, p * 128:(p + 1) * 128], q16[:, p * 128:(p + 1) * 128], identb[:],
                                      start=True, stop=True), [bq16, bK], [tbb])
            QT, bQT = T.QTr.next()
            evac(QT[0:64, 0, :], tb[0:64, :], [tbb], [bQT])
            evac(QT[64:128, 1, :], tb[64:128, :], [tbb], [bQT])
            h_["QT"] = (QT, bQT)
            h_["yp"] = T.ypr.next()
            HB[i] = h_

        def CB(i):
            h_ = HB[i]
            gt, bg, rw, brw = h_["g"]
            ktok, bkt, vaug, bva, kT, bkT_, qT, bqT = h_["mi"]
            soa, bsoa = h_["soa"]
            yp_, byp = h_["yp"]
            if is_prompt and i == 3:
                reset_state(1, mstate)
            mo, bmo = seq_scalar_step(1, gt, bg, mstate)
            hb, bhb = T.hbr.next()
            full_step(1, gt, bg, rw, brw, mo, bmo, qT, bqT, kT, bkT_, ktok, bkt, vaug, bva, hb, bhb)
            dve(lambda E: E.tensor_tensor(out=hb[:], in0=hb[:], in1=hfs[:, i, :], op=ALU.add), [bhb, bhf[i]], [bhb])
            headnorm(hb, bhb, 4, 128, ghnb[:].rearrange("p (a b) -> p a b", a=4), sq_ring=T.sqCr)
            dve(lambda E: E.tensor_tensor(out=yp_[:, 0:512], in0=hb[:], in1=soa[:], op=ALU.mult), [bhb, bsoa], [byp])
            if is_prompt and i % 2 == 0:
                write_state(seq_i + i // 2, 1, mstate)

        def GB(i):
            h_ = HB[i]
            QT, bQT = h_["QT"]
            yp_, byp = h_["yp"]
            Ob = [pst[3][:, 0:512], pst[3][:, 512:1024]]
            Obb = [psb[6], psb[7]]
            kts = [2 * (i // 2), 2 * (i // 2) + 1] if is_prompt else list(range(nkt))
            its = [(kt, g) for kt in kts for g in range(2)]
            pend = None
            for it in its + [None]:
                cur = None
                if it is not None:
                    kt, g = it
                    sb_, sbb = ps1()
                    pe(lambda E: E.matmul(sb_[:], KTg[:, kt * 128:(kt + 1) * 128], QT[:, g, :], start=True, stop=True),
                       [bKT[kt], bQT], [sbb])
                    PG, bPG = T.PGr.next()
                    act(lambda E: E.activation(out=PG[:], in_=sb_[:], func=AF.Exp, scale=0.125), [sbb], [bPG])
                    cur = (kt, g, PG, bPG)
                if pend is not None:
                    kt0, g0, PG0, bPG0 = pend
                    for p in range(4):
                        pe(lambda E: E.matmul(Ob[g0][:, p * 65:(p + 1) * 65], PG0[:, p * 128:(p + 1) * 128],
                                              Vg[:, kt0, g0, :], start=(kt0 == kts[0] and p == 0), stop=(kt0 == kts[-1]),
                                              skip_group_check=True), [bPG0, bVg[kt0]], [Obb[g0]])
                pend = cur
            for g in range(2):
                st, bst = str_.next()
                o3 = Ob[g][:, 0:260].rearrange("p (a b) -> p a b", a=4)
                dve(lambda E: E.reciprocal(out=st[:, 0:4], in_=o3[:, :, 64]), [Obb[g]], [bst])
                dve(lambda E: E.tensor_tensor(out=yp_[:, 512 + g * 256:768 + g * 256].rearrange("p (a b) -> p a b", a=4),
                                              in0=o3[:, :, 0:64], in1=st[:, 0:4].unsqueeze(2).to_broadcast([128, 4, 64]),
                                              op=ALU.mult), [Obb[g], bst], [byp])

        def FB(i):
            h_ = HB.pop(i)
            yp_, byp = h_["yp"]
            hT, bh = hTs[:, i], bhT[i]
            sz, bsz = T.szr.next()
            for hf in range(2):
                zbank, zbb = proj(hT, bh, C_Z + hf * 512, 512)
                silu_from_psum(zbank, zbb, sz[:, hf * 512:(hf + 1) * 512], bsz)
            yb, byb = T.ybr.next()
            dve(lambda E: E.tensor_tensor(out=yb[:], in0=yp_[:], in1=sz[:], op=ALU.mult), [byp, bsz], [byb])
            pp, pb = ps2()
            for kt in range(8):
                pe(lambda E: E.matmul(pp[:, kt * 128:(kt + 1) * 128], yb[:, kt * 128:(kt + 1) * 128], identb[:],
                                      start=True, stop=True), [byb, bK], pb)
            yT, byT = T.yTr.next()
            act(lambda E: E.activation(out=yT[:, 0:4, :], in_=pp[:, 0:512].rearrange("p (a b) -> p a b", a=4),
                                       func=AF.Copy), pb, [byT])
            dve(lambda E: E.tensor_copy(out=yT[:, 4:8, :], in_=pp[:, 512:1024].rearrange("p (a b) -> p a b", a=4)),
                pb, [byT])
            po, pob = ps2()
            for hf in range(2):
                for kt in range(8):
                    pe(lambda E: E.matmul(po[:, hf * 512:(hf + 1) * 512], yT[:, kt, :], W0o[:, kt, hf * 512:(hf + 1) * 512],
                                          start=(kt == 0), stop=(kt == 7)), [byT, bW0ok[kt]], [pob[hf]])
            residual_out(po, pob, xin[chunks[i]], s_x1[x1_base + i], yb, byb)

        def FP(t):
            if 0 <= t - 2 < n:
                FB(order[t - 2])
            if t < n:
                PB(order[t])

        for t in range(n + 2):
            tasks = [rec_task("a", FP, t)]
            if 0 <= t - 1 < n:
                tasks.append(rec_task("b", CB, order[t - 1]))
                tasks.append(rec_task("c", GB, order[t - 1]))
            m.replay(tasks)
        m.barrier()
        bs_.close()

    def residual_out(po, pob, src, dst, junk, bjunk, src_reads=(), final=False):
        st, bst = str_.next()
        dve(lambda E: E.memset(st[:, 0:1], 0.0), [], [bst])
        act(lambda E: E.activation(out=junk[:], in_=po[:], func=AF.Square, accum_out=st[:, 0:1]), pob + [bst],
            [bjunk, bst])
        rstd_from_ssq(st, bst, 1, 1.0 / D)
        w = T.outr.tiles[0].shape[1]
        for c0 in range(0, D, w):
            xo, bxo = T.outr.next()
            m.dma("sp", xo[:], src[:, c0:c0 + w], reads=list(src_reads), writes=[bxo])
            yp_, byp = T.tmpr.next()
            dve(lambda E: E.scalar_tensor_tensor(out=yp_[:], in0=po[:, c0:c0 + w], scalar=st[:, 0:1],
                                                 in1=T.modgg[:, c0:c0 + w], op0=ALU.mult, op1=ALU.mult),
                pob + [bst, T.bmod], [byp])
            m.op("pool", lambda E: E.tensor_tensor(out=xo[:], in0=xo[:], in1=yp_[:], op=ALU.add), [bxo, byp], [bxo])
            m.dma("sp", dst[:, c0:c0 + w], xo[:], reads=[bxo], writes=[] if final else [bX1])


    def layer1():
        if conv_state["in"] < len(conv_chunks):
            with ExitStack() as ces:
                cr = ring(ces, "cvr1", [128, CV], BF16, 4)
                while conv_state["pend"] or conv_state["in"] < len(conv_chunks):
                    conv_out()
                    conv_in(cr)
                m.barrier()
        L1 = ExitStack()
        hT1 = sb(L1, "hT1", [128, NIN, 8, 128], BF16)
        bh1 = [Buf() for _ in range(NIN)]
        K1T = sb(L1, "K1T", [128, 8, NIN * 128], BF16)
        bK1 = [Buf() for _ in range(NIN)]
        V1 = sb(L1, "V1", [128, NIN, 16, 65], BF16)
        bV1 = [Buf() for _ in range(NIN)]
        for i_ in range(NIN):
            dve(lambda E: E.memset(V1[:, i_], 1.0), [], [bV1[i_]])
        st1 = ring(L1, "st1", [128, 16], F32, 6)

        E2 = sb(L1, "E2", [128, 16, 14, 64], BF16)
        bE2 = Buf()
        Kc = sb(L1, "Kc", [128, 8, 256], BF16)
        Vc = sb(L1, "Vc", [128, 2, 16, 65], BF16)
        bKc = Buf()
        m.dma("pool", Kc[:], nkc, writes=[bKc])
        m.dma("pool", Vc[:], nvc.rearrange("t p h c -> p t h c"), writes=[bKc])
        with ExitStack() as es2:
            ebr = ring(es2, "ebr", [128, 2, 14, 64], F32, 2)
            for hq in range(8):
                et, bet = ebr.next()
                m.dma("sp", et[:], ebias[:, hq * 2:(hq + 1) * 2], writes=[bet])
                act(lambda E: E.activation(out=E2[:, hq * 2:(hq + 1) * 2], in_=et[:], func=AF.Exp), [bet], [bE2])
            m.barrier()

        def pass1(idxs, is_prompt, var):
            es = ExitStack()
            Wa = sb(es, "W1a", [128, 8, 2048], BF16)
            bWak = [Buf() for _ in range(8)]
            for kt in range(8):
                m.dma("sp", Wa[:, kt, :], s_w1[:, kt * 4096 + 1024:kt * 4096 + 3072], reads=bWscs, writes=[bWak[kt]])
            T.modg = sb(es, "modg1", [128, D], F32)
            T.mods = sb(es, "mods1", [128, D], F32)
            T.bmod = Buf()
            m.dma("sp", T.modg[:], s_mod[1, var, 0].partition_broadcast(128), reads=[bK], writes=[T.bmod])
            m.dma("sp", T.mods[:], s_mod[1, var, 1].partition_broadcast(128), reads=[bK], writes=[T.bmod])
            T.xr = ring(es, "xr1", [128, D], F32, 2)
            T.jr = ring(es, "jr1", [128, D], BF16, 1)
            T.hr = ring(es, "hr1", [128, D], BF16, 1)
            k32r = ring(es, "k32r", [128, D], F32, 2)
            k16r = ring(es, "k16r", [128, D], BF16, 2)
            v32r = ring(es, "v32r", [128, D], F32, 2) if is_prompt else k32r

            def A1(i):
                prenorm(s_x1[idxs[i]], hT1[:, i], bh1[i], src_reads=[bX1])

            def A2(i):
                hT, bh = hT1[:, i], bh1[i]
                seq_i, half = i // 2, i % 2
                k32, bk32 = k32r.next()
                k16, bk16 = k16r.next()
                for hf in range(2):
                    bank, bb = ps1()
                    with m.group():
                        for kt in range(8):
                            pe(lambda E: E.matmul(bank[:], hT[:, kt, :], Wa[:, kt, hf * 512:(hf + 1) * 512],
                                                  start=(kt == 0), stop=(kt == 7)), [bh, bWak[kt]], [bb])
                    if is_prompt:
                        act(lambda E: E.activation(out=k32[:, hf * 512:(hf + 1) * 512], in_=bank[:], func=AF.Copy),
                            [bb], [bk32])
                        m.op("pool", lambda E: E.tensor_copy(out=k16[:, hf * 512:(hf + 1) * 512],
                                                             in_=k32[:, hf * 512:(hf + 1) * 512]), [bk32], [bk16])
                    else:
                        dve(lambda E: E.tensor_copy(out=k16[:, hf * 512:(hf + 1) * 512], in_=bank[:]), [bb], [bk16])
                if is_prompt:
                    for h in range(16):
                        m.dma("sp", o_nk[seq_i, h, half * 128:(half + 1) * 128, :], k32[:, h * 64:(h + 1) * 64],
                              reads=[bk32])
                pp, pb = ps2()
                for hp in range(8):
                    pe(lambda E: E.matmul(pp[:, hp * 128:(hp + 1) * 128], k16[:, hp * 128:(hp + 1) * 128], identb[:],
                                          start=True, stop=True), [bk16, bK], pb)
                act(lambda E: E.activation(out=K1T[:, 0:4, i * 128:(i + 1) * 128],
                                           in_=pp[:, 0:512].rearrange("p (a b) -> p a b", a=4), func=AF.Copy),
                    pb, [bK1[i]])
                dve(lambda E: E.tensor_copy(out=K1T[:, 4:8, i * 128:(i + 1) * 128],
                                            in_=pp[:, 512:1024].rearrange("p (a b) -> p a b", a=4)), pb, [bK1[i]])
            def A3(i):
                hT, bh = hT1[:, i], bh1[i]
                seq_i, half = i // 2, i % 2
                v32, bv32 = v32r.next()
                for hf in range(2):
                    bank, bb = ps1()
                    with m.group():
                        for kt in range(8):
                            pe(lambda E: E.matmul(bank[:], hT[:, kt, :], Wa[:, kt, 1024 + hf * 512:1024 + (hf + 1) * 512],
                                                  start=(kt == 0), stop=(kt == 7)), [bh, bWak[kt]], [bb])
                    if is_prompt:
                        act(lambda E: E.activation(out=v32[:, hf * 512:(hf + 1) * 512], in_=bank[:], func=AF.Copy),
                            [bb], [bv32])
                        m.op("pool", lambda E: E.tensor_copy(out=V1[:, i, hf * 8:(hf + 1) * 8, 0:64],
                                                             in_=v32[:, hf * 512:(hf + 1) * 512].rearrange("p (h e) -> p h e", h=8)),
                             [bv32], [bV1[i]])
                    else:
                        dve(lambda E: E.tensor_copy(out=V1[:, i, hf * 8:(hf + 1) * 8, 0:64],
                                                    in_=bank.rearrange("p (h e) -> p h e", h=8)), [bb], [bV1[i]])
                if is_prompt:
                    for h in range(16):
                        m.dma("sp", o_nv[seq_i, h, half * 128:(half + 1) * 128, :], v32[:, h * 64:(h + 1) * 64],
                              reads=[bv32])

            n_ = len(idxs)
            for t in range(n_ + 1):
                tasks = []
                if t < n_:
                    tasks.append(rec_task("a", A1, t))
                if 0 <= t - 1 < n_:
                    tasks.append(rec_task("b", A2, t - 1))
                    tasks.append(rec_task("c", A3, t - 1))
                m.replay(tasks)
            m.barrier()
            es.close()

        def pass2(idxs, is_prompt, var, odst):
            es = ExitStack()
            Wb = sb(es, "W1b", [128, 8, 2048], BF16)
            Wo = sb(es, "W1o", [128, 8, D], BF16)
            bWbq = [Buf() for _ in range(8)]; bWbz = [Buf() for _ in range(8)]; bWo = Buf()
            for kt in range(8):
                m.dma("sp", Wb[:, kt, 0:1024], s_w1[:, kt * 4096:kt * 4096 + 1024], reads=bWscs, writes=[bWbq[kt]])
                m.dma("sp", Wb[:, kt, 1024:2048], s_w1[:, kt * 4096 + 3072:kt * 4096 + 4096], reads=bWscs, writes=[bWbz[kt]])
            m.dma("sp", Wo[:].rearrange("p k c -> p (k c)"), s_w1o, reads=bWscs, writes=[bWo])
            T.modgg = sb(es, "modgg1", [128, D], F32)
            T.bmod = Buf()
            m.dma("sp", T.modgg[:], s_mod[1, var, 2].partition_broadcast(128), reads=[bK], writes=[T.bmod])
            T.outr = ring(es, "outr1", [128, 512], F32, 2)
            T.tmpr = ring(es, "tmpr1", [128, 512], F32, 1)
            q16r = ring(es, "q16r", [128, D], BF16, 1)
            Q1Tr = ring(es, "Q1Tr", [128, 2, 8, 128], BF16, 2)
            for t_ in Q1Tr.tiles:
                dve(lambda E: E.memset(t_[64:128, 0], 0.0), [], Q1Tr.bufs)
                dve(lambda E: E.memset(t_[0:64, 1], 0.0), [], Q1Tr.bufs)
            szr = ring(es, "szr1", [128, D], BF16, 1)
            y1r = ring(es, "y1r", [128, D], BF16, 2)
            ybr = ring(es, "ybr1", [128, D], BF16, 1)
            yTr = ring(es, "yTr1", [128, 8, 128], BF16, 1)
            Pr = ring(es, "Pr", [128, 1024], BF16, 2) if is_prompt else ring(es, "Pr", [128, 896], BF16, 3)
            HQ = {}

            def B1(i):
                hT, bh = hT1[:, i], bh1[i]
                q16, bq16 = q16r.next()
                for hf in range(2):
                    bank, bb = ps1()
                    with m.group():
                        for kt in range(8):
                            pe(lambda E: E.matmul(bank[:], hT[:, kt, :], Wb[:, kt, hf * 512:(hf + 1) * 512],
                                                  start=(kt == 0), stop=(kt == 7)), [bh, bWbq[kt]], [bb])
                    dve(lambda E: E.tensor_copy(out=q16[:, hf * 512:(hf + 1) * 512], in_=bank[:]), [bb], [bq16])
                pp, pb = ps2()
                for hp in range(8):
                    pe(lambda E: E.matmul(pp[:, hp * 128:(hp + 1) * 128], q16[:, hp * 128:(hp + 1) * 128], identb[:],
                                          start=True, stop=True), [bq16, bK], pb)
                Q1T, bQ1 = Q1Tr.next()
                for hh_ in range(2):
                    rs = slice(hh_ * 64, hh_ * 64 + 64)
                    act(lambda E: E.activation(out=Q1T[rs, hh_, 0:4, :],
                                               in_=pp[rs, 0:512].rearrange("p (a b) -> p a b", a=4), func=AF.Copy),
                        pb, [bQ1])
                    dve(lambda E: E.tensor_copy(out=Q1T[rs, hh_, 4:8, :],
                                                in_=pp[rs, 512:1024].rearrange("p (a b) -> p a b", a=4)), pb, [bQ1])
                HQ[i] = (Q1T, bQ1, y1r.next())

            def o_post(hg, Ob, Obb, y1, by1):
                st, bst = str_.next()
                o3 = Ob[:, 0:260].rearrange("p (a b) -> p a b", a=4)
                dve(lambda E: E.reciprocal(out=st[:, 0:4], in_=o3[:, :, 64]), [Obb], [bst])
                dve(lambda E: E.tensor_tensor(out=y1[:, hg * 256:(hg + 1) * 256].rearrange("p (a b) -> p a b", a=4),
                                              in0=o3[:, :, 0:64],
                                              in1=st[:, 0:4].unsqueeze(2).to_broadcast([128, 4, 64]), op=ALU.mult),
                    [Obb, bst], [by1])

            def B2(i):
                Q1T, bQ1, (y1, by1) = HQ[i]
                if is_prompt:
                    kts = [2 * (i // 2), 2 * (i // 2) + 1]
                    for hg in range(4):
                        Ob = pst[3][:, (hg % 2) * 512:(hg % 2) * 512 + 512]
                        Obb = psb[6 + hg % 2]
                        sp_, spb = ps2()
                        for hl in range(4):
                            h = hg * 4 + hl
                            hp, hh = h // 2, h % 2
                            for kk, kt in enumerate(kts):
                                c0 = ((hl % 2) * 4 + (hl // 2) * 2 + kk) * 128
                                pe(lambda E: E.matmul(sp_[:, c0:c0 + 128], K1T[:, hp, kt * 128:(kt + 1) * 128],
                                                      Q1T[:, hh, hp, :], start=True, stop=True),
                                   [bK1[kt], bQ1], spb)
                        P, bP = Pr.next()
                        act(lambda E: E.activation(out=P[:], in_=sp_[:], func=AF.Exp, scale=0.125), spb, [bP])
                        for hl in range(4):
                            h = hg * 4 + hl
                            for kk, kt in enumerate(kts):
                                c0 = ((hl % 2) * 4 + (hl // 2) * 2 + kk) * 128
                                pe(lambda E: E.matmul(Ob[:, hl * 65:(hl + 1) * 65], P[:, c0:c0 + 128], V1[:, kt, h, :],
                                                      start=(kk == 0), stop=(kk == 1)), [bP, bV1[kt]], [Obb])
                        o_post(hg, Ob, Obb, y1, by1)
                    return
                r0 = 2 * i
                if 2 <= i <= 9:
                    tiles = [r0 - 4 + 2 * t for t in range(5)]
                else:
                    base = 0 if i < 2 else 16
                    tiles = [base + 2 * t for t in range(4)]
                nt = len(tiles)
                s0 = tiles[0] - r0 + 7
                E7 = E2[:].rearrange("p h (a b) e -> p h a b e", b=2)

                def na_stage1(u):
                    hg, hl = u
                    h = hg * 4 + hl
                    hp, hh = h // 2, h % 2
                    banks = [ps1(), ps1()]
                    P, bP = Pr.next()
                    segs = [("loc", a_) for a_ in tiles] + [("ctx", 0), ("ctx", 1)]
                    for si, (kind, a_) in enumerate(segs):
                        bank, bb = banks[si // 4]
                        c0 = (si % 4) * 128
                        if kind == "loc":
                            pe(lambda E: E.matmul(bank[:, c0:c0 + 128], K1T[:, hp, a_ * 64:a_ * 64 + 128],
                                                  Q1T[:, hh, hp, :], start=True, stop=True),
                               [bK1[a_ // 2], bQ1], [bb])
                        else:
                            pe(lambda E: E.matmul(bank[:, c0:c0 + 128], Kc[:, hp, a_ * 128:(a_ + 1) * 128],
                                                  Q1T[:, hh, hp, :], start=True, stop=True), [bKc, bQ1], [bb])
                    n0 = min(4, len(segs)) * 128
                    n1 = (len(segs) - 4) * 128
                    act(lambda E: E.activation(out=P[:, 0:n0], in_=banks[0][0][:, 0:n0], func=AF.Exp, scale=0.125),
                        [banks[0][1]], [bP])
                    act(lambda E: E.activation(out=P[:, n0:n0 + n1], in_=banks[1][0][:, 0:n1], func=AF.Exp, scale=0.125),
                        [banks[1][1]], [bP])
                    P4 = P[:, 0:nt * 128].rearrange("p (t r e) -> p t r e", t=nt, r=2)
                    for ir in range(2):
                        sl = s0 - ir
                        m.op("dve" if ir == 0 else "pool",
                             lambda E: E.tensor_tensor(out=P4[:, :, ir, :], in0=P4[:, :, ir, :],
                                                       in1=E7[:, h, sl // 2:sl // 2 + nt, sl % 2, :], op=ALU.mult),
                             [bP, bE2], [bP])
                    if nt == 5:
                        m.op("pool", lambda E: E.memset(P[0:64, 64:128], 0.0), [bP], [bP])
                        m.op("pool", lambda E: E.memset(P[:, 512:576], 0.0), [bP], [bP])
                        m.op("pool", lambda E: E.memset(P[64:128, 576:640], 0.0), [bP], [bP])
                    return (u, segs, P, bP)

                def na_stage2(rec):
                    (hg, hl), segs, P, bP = rec
                    h = hg * 4 + hl
                    Ob = pst[3][:, (hg % 2) * 512:(hg % 2) * 512 + 512]
                    Obb = psb[6 + hg % 2]
                    for si, (kind, a_) in enumerate(segs):
                        rhs = V1[:, a_ // 2, h, :] if kind == "loc" else Vc[:, a_, h, :]
                        pe(lambda E: E.matmul(Ob[:, hl * 65:(hl + 1) * 65], P[:, si * 128:(si + 1) * 128], rhs,
                                              start=(si == 0), stop=(si == len(segs) - 1)),
                           [bP, bV1[a_ // 2] if kind == "loc" else bKc], [Obb])
                    if hl == 3:
                        o_post(hg, Ob, Obb, y1, by1)

                pend = None
                for u in [(hg, hl) for hg in range(4) for hl in range(4)] + [None]:
                    cur = na_stage1(u) if u is not None else None
                    if pend is not None:
                        na_stage2(pend)
                    pend = cur

            def B3(i):
                hT, bh = hT1[:, i], bh1[i]
                Q1T, bQ1, (y1, by1) = HQ.pop(i)
                sz, bsz = szr.next()
                for hf in range(2):
                    bank, bb = ps1()
                    with m.group():
                        for kt in range(8):
                            pe(lambda E: E.matmul(bank[:], hT[:, kt, :], Wb[:, kt, 1024 + hf * 512:1024 + (hf + 1) * 512],
                                                  start=(kt == 0), stop=(kt == 7)), [bh, bWbz[kt]], [bb])
                    silu_from_psum(bank, bb, sz[:, hf * 512:(hf + 1) * 512], bsz)
                yb, byb = ybr.next()
                dve(lambda E: E.tensor_tensor(out=yb[:], in0=y1[:], in1=sz[:], op=ALU.mult), [by1, bsz], [byb])
                pp, pb = ps2()
                for kt in range(8):
                    pe(lambda E: E.matmul(pp[:, kt * 128:(kt + 1) * 128], yb[:, kt * 128:(kt + 1) * 128], identb[:],
                                          start=True, stop=True), [byb, bK], pb)
                yT, byT = yTr.next()
                act(lambda E: E.activation(out=yT[:, 0:4, :], in_=pp[:, 0:512].rearrange("p (a b) -> p a b", a=4),
                                           func=AF.Copy), pb, [byT])
                dve(lambda E: E.tensor_copy(out=yT[:, 4:8, :], in_=pp[:, 512:1024].rearrange("p (a b) -> p a b", a=4)),
                    pb, [byT])
                po, pob = ps2()
                for hf in range(2):
                    for kt in range(8):
                        pe(lambda E: E.matmul(po[:, hf * 512:(hf + 1) * 512], yT[:, kt, :], Wo[:, kt, hf * 512:(hf + 1) * 512],
                                              start=(kt == 0), stop=(kt == 7)), [byT, bWo], [pob[hf]])
                residual_out(po, pob, s_x1[idxs[i]], odst[i], yb, byb, src_reads=[bX1], final=True)

            n_ = len(idxs)
            for t in range(n_ + 2):
                tasks = []
                if t < n_:
                    tasks.append(rec_task("a", B1, t))
                if 0 <= t - 1 < n_:
                    tasks.append(rec_task("b", B2, t - 1))
                if 0 <= t - 2 < n_:
                    tasks.append(rec_task("c", B3, t - 2))
                m.replay(tasks)
            m.barrier()
            es.close()

        pass1([0, 1, 2, 3], True, 0)
        if stage >= 5:
            pass2([0, 1, 2, 3], True, 0, [o_yp[i] for i in range(4)])
        sidx = [NPC + i for i in range(NIN)]
        if stage >= 7:
            pass1(sidx, False, 1)
        if stage >= 8:
            pass2(sidx, False, 1, [o_ys[i] for i in range(NIN)])
        L1.close()


    bX1 = Buf()
    run_sequence([0, 1, 2, 3], True, 0, 0, 0, 0)
    if stage >= 3 and not no_sample:
        run_sequence([NPC + NOUT + i for i in range(NIN)], False, None, NOUT, 1, NPC)
    if stage < 4:
        tr = ring(L0, "dbgx", [128, D], F32, 2)
        for i in range(NPC):
            t, b_ = tr.next()
            m.dma("sp", t[:], s_x1[i], reads=[bX1], writes=[b_])
            m.dma("sp", o_yp[i], t[:], reads=[b_])
        if stage >= 3:
            for i in range(NIN):
                t, b_ = tr.next()
                m.dma("sp", t[:], s_x1[NPC + i], reads=[bX1], writes=[b_])
                m.dma("sp", o_ys[i], t[:], reads=[b_])
    m.barrier()
    L0.close()
    LW.close()
    if stage >= 4:
        layer1()
    m.finish()
    return nc, m


def _kt(w):
    k, n = w.shape
    return np.ascontiguousarray(w.reshape(k // 128, 128, n).transpose(1, 0, 2))


def _consts():
    i = np.arange(128)
    ut = (i[:, None] <= i[None, :]).astype(np.float32)
    lt = (i[:, None] >= i[None, :]).astype(np.float32)
    sel = np.zeros((8, 8, 128), np.float32)
    for j in range(8):
        sel[j, j, :] = 1.0
    pick = np.zeros((2, 128, 128), np.float32)
    pick[0, 127, :] = 1.0
    pick[1, 0, :] = 1.0
    return dict(cident=np.eye(128, dtype=np.float32), cut=ut, clt=lt,
                cmut=np.where(ut > 0, 0.0, NEG).astype(np.float32), cmlt=np.where(lt > 0, 0.0, NEG).astype(np.float32),
                csel=sel.reshape(8, 1024), cones=np.ones((128, 128), np.float32), cpick=pick)


def _prep(inp):
    f = lambda k: np.asarray(inp[k], dtype=np.float32)
    xp, xs = f("x_prompt"), f("x_sample")
    w_in = f("w_in_ab")[0]
    gperm = [0, 1, 2, 3, 8, 9, 10, 11, 4, 5, 6, 7, 12, 13, 14, 15]
    gates = w_in[:, 2048:2064][:, gperm]
    w0 = np.concatenate([w_in[:, 512:1024], w_in[:, 1024:1536], w_in[:, 2576:2704], w_in[:, 2704:2832], gates,
                         w_in[:, 0:512], w_in[:, 1536:2048],
                         w_in[:, 2064:2576].reshape(D, 2, 4, 64).transpose(0, 2, 1, 3).reshape(D, 512),
                         w_in[:, 2832:3856]], axis=1)
    assert w0.shape[1] == NW0
    shared = dict(
        w0=_kt(w0), w0o=_kt(f("w_out_ab")[0]), w1=_kt(f("w_in_c")[0]), w1o=_kt(f("w_out_c")[0]),
        wmod=np.stack([_kt(f("w_mod")[l]) for l in range(2)]), bmod=f("b_mod"), gpre=f("g_pre"), gpost=f("g_post"),
        bgate=f("b_gates_ab")[0][gperm], ghn=f("g_hnorm_a")[0],
        gqk=np.stack([f("g_qnorm_b")[0], f("g_knorm_b")[0]]),
    )
    shared.update(_consts())
    t = np.arange(4096)
    fr = (10000.0 ** (-np.arange(16, dtype=np.float32) / 16)).astype(np.float32)
    ar = (t // 64).astype(np.float32)[:, None] * fr
    ac = (t % 64).astype(np.float32)[:, None] * fr
    cos64 = np.concatenate([np.cos(ar), np.cos(ar), np.cos(ac), np.cos(ac)], 1).astype(np.float32)
    sin64 = np.concatenate([-np.sin(ar), np.sin(ar), -np.sin(ac), np.sin(ac)], 1).astype(np.float32)
    rpb = f("rpb_c")[0]
    ck = np.arange(64)[:, None]
    cq = np.arange(64)[None, :]
    cs = np.clip(cq - 8, 0, 48)
    valid = (ck >= cs) & (ck < cs + 16)
    dcol = np.clip(ck - cq + 15, 0, 30)
    eb = np.full((2, 64, 16, 14, 64), NEG, np.float32)
    for dr0 in range(14):
        for jr in range(2):
            eb[jr, :, :, dr0, :] = np.where(valid[:, None, :], rpb[:, dr0 + jr, :][:, dcol].transpose(1, 0, 2), NEG)
    shared["ebias"] = eb.reshape(128, 16, 14, 64)
    maps = []
    for c in range(NCORES):
        b, q = c // 4, c % 4
        c0 = STARTS[q] // 128
        inside = list(range(c0, c0 + NIN))
        before = list(range(0, c0))
        after = list(range(c0 + NIN, 32))[::-1]
        outside = before + after
        xsb = xs[b].reshape(32, 128, D)
        xin = np.concatenate([xp[2 * c].reshape(2, 128, D), xp[2 * c + 1].reshape(2, 128, D), xsb[outside], xsb[inside]], 0)
        order = outside + inside
        fl = np.zeros((NOUT, 8), np.float32)
        fl[:len(before), 0:4] = 1.0
        fl[len(before):, 4:8] = 1.0
        flg = np.stack([fl.reshape(-1), ((1.0 - fl) * -1e30).reshape(-1)]).astype(np.float32)
        Cst = f("state_mlstm_C")[b, 0]
        nst = f("state_mlstm_n")[b, 0]
        ct0 = np.concatenate([Cst.transpose(0, 1, 3, 2), nst[..., None]], -1).reshape(8, 128, 129)
        gk = f("cache_gqa_k")[b, 0]
        gv = f("cache_gqa_v")[b, 0]
        gvc = np.ones((2, 128, 2, 65), np.float32)
        gvc[..., 0:64] = gv.reshape(2, 2, 128, 64).transpose(1, 2, 0, 3)
        nk = f("cache_na_k")[b, 0]
        nv = f("cache_na_v")[b, 0]
        nvc = np.ones((2, 128, 16, 65), np.float32)
        nvc[..., 0:64] = nv.reshape(16, 2, 128, 64).transpose(1, 2, 0, 3)
        d = dict(shared)
        d.update(
            xin=np.ascontiguousarray(xin),
            cvT=np.ascontiguousarray(np.stack([f("c_ctx"), f("c")[b]], -1).reshape(8, 128, 2).transpose(1, 0, 2)),
            ropec=cos64.reshape(32, 128, 64)[order], ropes=sin64.reshape(32, 128, 64)[order],
            ct0=np.ascontiguousarray(ct0), m0=f("state_mlstm_m")[b, 0].reshape(8), flg=flg,
            gkc=np.ascontiguousarray(gk.transpose(0, 2, 1).reshape(128, 256)), gvc=gvc,
            nkc=np.ascontiguousarray(nk.reshape(8, 2, 256, 64).transpose(1, 3, 0, 2).reshape(128, 8, 256)), nvc=nvc,
        )
        maps.append({k: np.ascontiguousarray(v, dtype=np.float32) for k, v in d.items()})
    return maps


_CACHE = {}


def kernel(**inputs):
    stage = inputs.pop("_stage", 99)
    maps = _prep(inputs)
    if stage not in _CACHE:
        _CACHE[stage] = build(stage)
    nc, _ = _CACHE[stage]
    res = run_bass_kernel_spmd(nc, maps, core_ids=list(range(NCORES)))
    R = res.results
    yp = np.zeros((16, 256, D), np.float32)
    ys = np.zeros((2, 4096, D), np.float32)
    oC = np.zeros((16, 1, 2, 4, 128, 128), np.float32)
    on = np.zeros((16, 1, 2, 4, 128), np.float32)
    om = np.zeros((16, 1, 2, 4), np.float32)
    ogk = np.zeros((16, 1, 2, 256, 64), np.float32)
    ogv = np.zeros((16, 1, 2, 256, 64), np.float32)
    onk = np.zeros((16, 1, 16, 256, 64), np.float32)
    onv = np.zeros((16, 1, 16, 256, 64), np.float32)
    for c in range(NCORES):
        r = R[c]
        b, q = c // 4, c % 4
        yp[2 * c:2 * c + 2] = r["o_yp"].reshape(2, 256, D)
        loc = 1024 * q - STARTS[q]
        ys[b, 1024 * q:1024 * (q + 1)] = r["o_ys"].reshape(NIN * 128, D)[loc:loc + 1024]
        oC[2 * c:2 * c + 2, 0] = r["o_C"].reshape(2, 2, 4, 128, 128)
        on[2 * c:2 * c + 2, 0] = r["o_n"].reshape(2, 2, 4, 128)
        om[2 * c:2 * c + 2, 0] = r["o_m"].reshape(2, 2, 4)
        ogk[2 * c:2 * c + 2, 0] = r["o_gk"]
        ogv[2 * c:2 * c + 2, 0] = r["o_gv"]
        onk[2 * c:2 * c + 2, 0] = r["o_nk"]
        onv[2 * c:2 * c + 2, 0] = r["o_nv"]
    return (yp, ys, oC, on, om, ogk, ogv, onk, onv)
```

```python
import numpy as np
import concourse.bass as bass
import concourse.mybir as mybir
from concourse.bass_utils import run_bass_kernel_spmd
from contextlib import ExitStack

F32 = mybir.dt.float32
BF16 = mybir.dt.bfloat16
AF = mybir.ActivationFunctionType
ALU = mybir.AluOpType
AX = mybir.AxisListType

DBG = 99
NCORES = 8
D = 1024
EPS = 1e-6
NEG = -30000.0
NPC = 4
NOUT = 20
NIN = 12
NCH = NPC + NOUT + NIN
STARTS = [0, 768, 1792, 2560]
C_KA, C_VA, C_KV, C_QA, C_OA, C_QB, C_Z = 0, 512, 1024, 1296, 1808, 2320, 2832
NW0 = 3856


class Buf:
    __slots__ = ("w", "r", "excl")

    def __init__(self, excl=False):
        self.w = None
        self.r = []
        self.excl = excl


class _Cap:
    def __init__(self):
        self.call = None

    def __getattr__(self, name):
        def f(*a, **kw):
            self.call = (name, a, kw)
            return None
        return f


class MK:
    SEM_ROLL = 8000

    def __init__(self, nc, n_dma=40):
        self.nc = nc
        self.es = ExitStack()
        self.eng = {"pe": nc.tensor, "act": nc.scalar, "dve": nc.vector, "pool": nc.gpsimd, "sp": nc.sync}
        self.sem = {}
        self.cnt = {}
        self.seen = {e: {} for e in self.eng}
        self.nsem = 0
        for e in self.eng:
            self._newsem(e)
        n_sw = 16
        self.dma_sems = [self.es.enter_context(nc.semaphore("dq%d" % i)) for i in range(n_dma + n_sw)]
        self.dma_val = [0] * (n_dma + n_sw)
        self.dma_rng = {"hw": (0, n_dma), "sw": (n_dma, n_sw)}
        self.dma_i = {"hw": 0, "sw": 0}
        self.n_inst = {e: 0 for e in self.eng}
        self.last = {}
        self.rec = None
        self.gid = None
        self.gctr = 0

    def _newsem(self, e):
        self.nsem += 1
        self.sem[e] = self.es.enter_context(self.nc.semaphore("s_%s_%d" % (e, self.nsem)))
        self.cnt[e] = 0

    def _waits(self, e, reads, writes, attach=False):
        E = self.eng[e]
        toks = []
        for b in reads:
            if b.w is not None:
                toks.append(b.w)
            if b.excl:
                toks.extend(t for t in b.r if t[2] != e)
        for b in writes:
            if b.w is not None:
                toks.append(b.w)
            toks.extend(b.r)
        seen = self.seen[e]
        need = {}
        for (sem, val, src) in toks:
            if src == e and e == "pe":
                continue
            k = id(sem)
            if seen.get(k, 0) >= val:
                continue
            if k not in need or need[k][1] < val:
                need[k] = (sem, val)
        items = list(need.items())
        for k, (sem, val) in items:
            seen[k] = val
        if attach and items:
            for k, (sem, val) in items[:-1]:
                E.wait_ge(sem, val)
                self.n_inst[e] += 1
            return items[-1][1]
        for k, (sem, val) in items:
            E.wait_ge(sem, val)
            self.n_inst[e] += 1
        return None

    def _done(self, tok, reads, writes):
        for b in reads:
            b.r.append(tok)
            if len(b.r) > 48:
                mx = {}
                for t in b.r:
                    k = id(t[0])
                    if k not in mx or mx[k][1] < t[1]:
                        mx[k] = t
                b.r = list(mx.values())
        for b in writes:
            b.w = tok
            b.r = []

    def group(self):
        mk = self

        class _G:
            def __enter__(self_):
                mk.gctr += 1
                self_.old = mk.gid
                mk.gid = mk.gctr

            def __exit__(self_, *a):
                mk.gid = self_.old
        return _G()

    def record(self, f, *a):
        old = self.rec
        self.rec = []
        f(*a)
        ops = self.rec
        self.rec = old
        return ops

    def replay(self, tasks):
        assert self.rec is None
        idx = [0] * len(tasks)
        left = sum(len(t) for t in tasks)
        while left:
            k = min((k for k in range(len(tasks)) if idx[k] < len(tasks[k])),
                    key=lambda k: (idx[k] + 0.5) / len(tasks[k]))
            g0 = tasks[k][idx[k]][-1]
            while True:
                o = tasks[k][idx[k]]
                idx[k] += 1
                left -= 1
                if o[0] == "op":
                    _, e, (name, a, kw), r, w, _g = o
                    self.op(e, lambda E: getattr(E, name)(*a, **kw), r, w)
                else:
                    _, q, out, in_, r, w, _g = o
                    self.dma(q, out, in_, r, w)
                if g0 is None or idx[k] >= len(tasks[k]) or tasks[k][idx[k]][-1] != g0:
                    break

    def op(self, e, fn, reads=(), writes=()):
        if self.rec is not None:
            cap = _Cap()
            fn(cap)
            self.rec.append(("op", e, cap.call, tuple(reads), tuple(writes), self.gid))
            return None
        att = self._waits(e, reads, writes, attach=True)
        ins = fn(self.eng[e])
        if att is not None:
            ins._wait_ge(att[0], att[1])
        if self.cnt[e] >= self.SEM_ROLL:
            self._newsem(e)
        self.cnt[e] += 1
        ins.then_inc(self.sem[e], 1)
        self.n_inst[e] += 1
        tok = (self.sem[e], self.cnt[e], e)
        self.last[e] = tok
        self._done(tok, reads, writes)
        return tok

    def dma(self, q, out, in_, reads=(), writes=()):
        if self.rec is not None:
            self.rec.append(("dma", q, out, in_, tuple(reads), tuple(writes), None))
            return None
        kind = "sw" if q == "pool" else "hw"
        base, n = self.dma_rng[kind]
        slot = base + self.dma_i[kind] % n
        self.dma_i[kind] += 1
        sem = self.dma_sems[slot]
        pv = self.dma_val[slot]
        att = self._waits(q, reads, writes, attach=True)
        seen = self.seen[q]
        if pv > 0 and seen.get(id(sem), 0) < pv:
            self.eng[q].wait_ge(sem, pv)
            seen[id(sem)] = pv
        ins = self.eng[q].dma_start(out=out, in_=in_)
        if att is not None:
            ins._wait_ge(att[0], att[1])
        ins.then_inc(sem, 16)
        self.n_inst[q] += 1
        self.dma_val[slot] = pv + 16
        tok = (sem, pv + 16, "dma")
        self._done(tok, reads, writes)
        return tok

    def barrier(self):
        assert self.rec is None
        self._barrier()

    def _barrier(self):
        toks = list(self.last.values())
        for i, sem in enumerate(self.dma_sems):
            if self.dma_val[i] > 0:
                toks.append((sem, self.dma_val[i], "dma"))
        for e in self.eng:
            seen = self.seen[e]
            for (sem, val, src) in toks:
                if seen.get(id(sem), 0) < val:
                    self.eng[e].wait_ge(sem, val)
                    seen[id(sem)] = val

    def finish(self):
        self.barrier()
        self.es.close()


class Ring:
    def __init__(self, tiles):
        self.tiles = tiles
        self.bufs = [Buf() for _ in tiles]
        self.i = 0

    def next(self):
        k = self.i % len(self.tiles)
        self.i += 1
        return self.tiles[k], self.bufs[k]


def build(stage=99):
    no_sample = stage >= 10 and stage < 90
    if stage < 90:
        stage = stage % 10
    nc = bass.Bass("TRN2", target_bir_lowering=False)
    m = MK(nc)
    m.es.enter_context(nc.allow_low_precision(reason="bf16 matmul operands, fp32 accumulation"))

    def din(name, shape, dt=F32):
        return nc.dram_tensor(name, list(shape), dt, kind="ExternalInput").ap()

    def dout(name, shape, dt=F32):
        return nc.dram_tensor(name, list(shape), dt, kind="ExternalOutput").ap()

    def dscr(name, shape, dt=F32):
        return nc.dram_tensor(name, list(shape), dt, kind="Internal").ap()

    xin = din("xin", [NCH, 128, D])
    w0 = din("w0", [128, 8, NW0])
    w0o = din("w0o", [128, 8, D])
    w1 = din("w1", [128, 8, 4096])
    w1o = din("w1o", [128, 8, D])
    wmod = din("wmod", [2, 128, 8, 3072])
    bmod = din("bmod", [2, 3072])
    gpre = din("gpre", [2, D])
    gpost = din("gpost", [2, D])
    cvT = din("cvT", [128, 8, 2])
    bgate = din("bgate", [16])
    ghn = din("ghn", [512])
    gqk = din("gqk", [2, 64])
    ropec = din("ropec", [NOUT + NIN, 128, 64])
    ropes = din("ropes", [NOUT + NIN, 128, 64])
    ct0 = din("ct0", [8, 128, 129])
    m0 = din("m0", [8])
    flg = din("flg", [2, NOUT * 8])
    gkc = din("gkc", [128, 256])
    gvc = din("gvc", [2, 128, 2, 65])
    nkc = din("nkc", [128, 8, 256])
    nvc = din("nvc", [2, 128, 16, 65])
    ebias = din("ebias", [128, 16, 14, 64])
    cident = din("cident", [128, 128])
    cut = din("cut", [128, 128])
    clt = din("clt", [128, 128])
    cmut = din("cmut", [128, 128])
    cmlt = din("cmlt", [128, 128])
    csel = din("csel", [8, 8 * 128])
    cones = din("cones", [128, 128])
    cpick = din("cpick", [2, 128, 128])

    o_yp = dout("o_yp", [NPC, 128, D])
    o_ys = dout("o_ys", [NIN, 128, D])
    o_C = dout("o_C", [2, 8, 128, 128])
    o_n = dout("o_n", [2, 8, 128])
    o_m = dout("o_m", [2, 8])
    o_gk = dout("o_gk", [2, 2, 256, 64])
    o_gv = dout("o_gv", [2, 2, 256, 64])
    o_nk = dout("o_nk", [2, 16, 256, 64])
    o_nv = dout("o_nv", [2, 16, 256, 64])

    s_mod = dscr("s_mod", [2, 2, 3, D])
    s_x1 = dscr("s_x1", [NPC + NIN, 128, D])
    s_w1 = dscr("s_w1", [128, 8 * 4096], BF16)
    s_w1o = dscr("s_w1o", [128, 8 * D], BF16)
    w1f = w1.rearrange("p k c -> p (k c)")
    w1of = w1o.rearrange("p k c -> p (k c)")
    CV = 1024
    conv_chunks = [(w1f, s_w1, i * CV) for i in range(8 * 4096 // CV)] + [(w1of, s_w1o, i * CV) for i in range(8 * D // CV)]
    conv_state = {"in": 0, "out": 0, "pend": []}
    bWscs = []

    def conv_in(ring_):
        k = conv_state["in"]
        if k >= len(conv_chunks):
            return
        src, dst, off = conv_chunks[k]
        stg, bstg = ring_.next()
        m.dma("pool", stg[:], src[:, off:off + CV], writes=[bstg])
        conv_state["pend"].append((stg, bstg, dst, off))
        conv_state["in"] += 1

    def conv_out():
        if conv_state["pend"]:
            stg, bstg, dst, off = conv_state["pend"].pop(0)
            bc = Buf()
            bWscs.append(bc)
            m.dma("sp", dst[:, off:off + CV], stg[:], reads=[bstg], writes=[bc])
            conv_state["out"] += 1

    top = m.es

    uid = [0]

    def sb(es, name, shape, dt):
        uid[0] += 1
        return es.enter_context(nc.sbuf_tensor("%s_%d" % (name, uid[0]), list(shape), dt))

    pst = [top.enter_context(nc.psum_tensor("psd%d" % i, [128, 1024], F32)) for i in range(4)]
    psb = [Buf(excl=True) for _ in range(8)]
    pctr = [0, 0]

    pools = {"all": [0, 1, 2, 3, 4, 5], "a": [0, 1], "b": [2, 3], "c": [4, 5], "d": [6, 7]}
    pcur = ["all"]
    pcnt = {k: [0, 0] for k in pools}

    def ps1():
        banks = pools[pcur[0]]
        k = banks[pcnt[pcur[0]][0] % len(banks)]
        pcnt[pcur[0]][0] += 1
        return pst[k // 2][:, (k % 2) * 512:(k % 2) * 512 + 512], psb[k]

    def ps2():
        banks = pools[pcur[0]]
        nd = len(banks) // 2
        k = banks[0] // 2 + pcnt[pcur[0]][1] % nd
        pcnt[pcur[0]][1] += 1
        return pst[k], [psb[2 * k], psb[2 * k + 1]]

    def rec_task(pool, f, *a):
        old = pcur[0]
        pcur[0] = pool
        ops = m.record(f, *a)
        pcur[0] = old
        return ops

    identb = sb(top, "identb", [128, 128], BF16)
    identf = sb(top, "identf", [128, 128], F32)
    utf = sb(top, "utf", [128, 128], F32)
    ltf = sb(top, "ltf", [128, 128], F32)
    mut = sb(top, "mut", [128, 128], BF16)
    mlt = sb(top, "mlt", [128, 128], BF16)
    sel = sb(top, "sel", [8, 8 * 128], F32)
    onesf = sb(top, "onesf", [128, 128], F32)
    pick = sb(top, "pick", [128, 2, 128], F32)
    bgb = sb(top, "bgb", [128, 16], F32)
    ghnb = sb(top, "ghnb", [128, 512], F32)
    gqkb = sb(top, "gqkb", [128, 2, 64], F32)
    bK = Buf()
    m.dma("pool", identb[:], cident, writes=[bK])
    m.dma("sp", identf[:], cident, writes=[bK])
    m.dma("sp", utf[:], cut, writes=[bK])
    m.dma("sp", ltf[:], clt, writes=[bK])
    m.dma("pool", mut[:], cmut, writes=[bK])
    m.dma("pool", mlt[:], cmlt, writes=[bK])
    m.dma("sp", sel[:], csel, writes=[bK])
    m.dma("sp", onesf[:], cones, writes=[bK])
    m.dma("sp", pick[:], cpick.rearrange("a p c -> p a c"), writes=[bK])
    m.dma("sp", bgb[:], bgate.partition_broadcast(128), writes=[bK])
    m.dma("sp", ghnb[:], ghn.partition_broadcast(128), writes=[bK])
    m.dma("sp", gqkb[:], gqk.partition_broadcast(128), writes=[bK])

    def selj(j):
        return sel[:, j * 128:(j + 1) * 128]

    def act(fn, reads, writes):
        return m.op("act", fn, reads, writes)

    def dve(fn, reads, writes):
        return m.op("dve", fn, reads, writes)

    def pe(fn, reads, writes):
        return m.op("pe", fn, reads, writes)

    def evac(out_ap, in_ap, reads, writes, scale=None):
        if T.bph:
            if scale is None:
                dve(lambda E: E.tensor_copy(out=out_ap, in_=in_ap), reads, writes)
            else:
                dve(lambda E: E.tensor_scalar(out=out_ap, in0=in_ap, scalar1=scale, scalar2=None, op0=ALU.mult), reads, writes)
        else:
            if scale is None:
                act(lambda E: E.activation(out=out_ap, in_=in_ap, func=AF.Copy), reads, writes)
            else:
                act(lambda E: E.activation(out=out_ap, in_=in_ap, func=AF.Copy, scale=scale), reads, writes)

    def silu_from_psum(zbank, zbb, dst, bdst):
        tmp, btmp = T.tmpr.next()
        act(lambda E: E.activation(out=tmp[:], in_=zbank[:], func=AF.Exp, scale=-1.0), [zbb], [btmp])
        dve(lambda E: E.tensor_scalar(out=tmp[:], in0=tmp[:], scalar1=1.0, scalar2=None, op0=ALU.add), [btmp], [btmp])
        dve(lambda E: E.reciprocal(out=tmp[:], in_=tmp[:]), [btmp], [btmp])
        dve(lambda E: E.tensor_tensor(out=dst, in0=zbank[:], in1=tmp[:], op=ALU.mult), [zbb, btmp], [bdst])

    def rstd_from_ssq(st, bst, n, inv_n):
        dve(lambda E: E.tensor_scalar(out=st[:, 0:n], in0=st[:, 0:n], scalar1=inv_n, scalar2=EPS,
                                      op0=ALU.mult, op1=ALU.add), [bst], [bst])
        act(lambda E: E.activation(out=st[:, 0:n], in_=st[:, 0:n], func=AF.Ln), [bst], [bst])
        act(lambda E: E.activation(out=st[:, 0:n], in_=st[:, 0:n], func=AF.Exp, scale=-0.5), [bst], [bst])

    def ring(es, name, shape, dt, n):
        return Ring([sb(es, "%s%d" % (name, i), shape, dt) for i in range(n)])

    class NS:
        pass

    T = NS()
    T.bph = False
    class _StRings:
        def __init__(self):
            self.r = {k: ring(top, "st" + k, [128, 16], F32, 6) for k in pools}

        def next(self):
            return self.r[pcur[0]].next()

    str_ = _StRings()
    LW = ExitStack()
    W0 = sb(LW, "W0", [128, 8, NW0], BF16)
    W0o = sb(LW, "W0o", [128, 8, D], BF16)
    bW0k = [Buf() for _ in range(8)]
    bW0ok = [Buf() for _ in range(8)]

    with ExitStack() as p0:
        cv = sb(p0, "cv", [128, 8, 2], F32)
        cvb = sb(p0, "cvb", [128, 8, 2], BF16)
        bcv = Buf()
        m.dma("sp", cv[:], cvT, writes=[bcv])
        act(lambda E: E.activation(out=cvb[:], in_=cv[:], func=AF.Silu), [bcv], [bcv])
        wmt = [sb(p0, "wmt%d" % i, [128, 8, 512], BF16) for i in range(2)]
        wmr = Ring(wmt)
        mrow = sb(p0, "mrow", [2, 3072], F32)
        brow = sb(p0, "brow", [2, 3072], F32)
        gpr = sb(p0, "gpr", [2, D], F32)
        gpo = sb(p0, "gpo", [2, D], F32)
        drow = sb(p0, "drow", [2, 3, D], F32)
        bmr = Buf()
        bbr = Buf()
        bdr = Buf()
        for L in range(2):
            m.dma("sp", brow[:], bmod[L].partition_broadcast(2), writes=[bbr])
            m.dma("sp", gpr[:], gpre[L].partition_broadcast(2), writes=[bbr])
            m.dma("sp", gpo[:], gpost[L].partition_broadcast(2), writes=[bbr])
            for cb in range(6):
                wt, bw = wmr.next()
                m.dma("pool", wt[:], wmod[L][:, :, cb * 512:(cb + 1) * 512], writes=[bw])
                bank, bb = ps1()
                with m.group():
                    for kt in range(8):
                        pe(lambda E: E.matmul(bank[0:2, :], cvb[:, kt, :], wt[:, kt, :], start=(kt == 0), stop=(kt == 7)),
                           [bcv, bw], [bb])
                dve(lambda E: E.tensor_tensor(out=mrow[:, cb * 512:(cb + 1) * 512], in0=bank[0:2, :],
                                              in1=brow[:, cb * 512:(cb + 1) * 512], op=ALU.add), [bb, bbr], [bmr])
            dve(lambda E: E.scalar_tensor_tensor(out=drow[:, 0, :], in0=mrow[:, D:2 * D], scalar=1.0, in1=gpr[:],
                                                 op0=ALU.add, op1=ALU.mult), [bmr, bbr], [bdr])
            dve(lambda E: E.tensor_copy(out=drow[:, 1, :], in_=mrow[:, 0:D]), [bmr], [bdr])
            dve(lambda E: E.tensor_tensor(out=drow[:, 2, :], in0=mrow[:, 2 * D:3 * D], in1=gpo[:], op=ALU.mult),
                [bmr, bbr], [bdr])
            m.dma("sp", s_mod[L], drow[:], reads=[bdr], writes=[bK])
        for kt in range(8):
            m.dma("pool", W0[:, kt, :], w0[:, kt, :], writes=[bW0k[kt]])
        for kt in range(0, 8, 4):
            m.dma("pool", W0o[:, kt:kt + 4, :], w0o[:, kt:kt + 4, :], writes=bW0ok[kt:kt + 4])
        m.barrier()

    L0 = ExitStack()
    NKT = 34
    KTg = sb(L0, "KTg", [128, NKT * 128], BF16)
    Vg = sb(L0, "Vg", [128, NKT, 2, 65], BF16)
    bKT = [Buf() for _ in range(NKT)]
    bVg = [Buf() for _ in range(NKT)]
    dve(lambda E: E.memset(Vg[:], 1.0), [], bVg)
    hTs = sb(L0, "hTs", [128, NIN, 8, 128], BF16)
    bhT = [Buf() for _ in range(NIN)]
    hfs = sb(L0, "hfs", [128, NIN, 512], BF16)
    bhf = [Buf() for _ in range(NIN)]
    CT = sb(L0, "CT", [128, 8, 129], F32)
    CTb = sb(L0, "CTb", [128, 8, 129], BF16)
    bCT = [Buf() for _ in range(8)]
    bCTb = [Buf() for _ in range(8)]
    mring = [Ring([sb(L0, "mbc%d_%d" % (d, i), [128, 4], F32) for i in range(3)]) for d in range(2)]
    flgt = sb(L0, "flgt", [128, 2, NOUT * 8], F32)
    bflg = Buf()
    m.dma("sp", flgt[:], flg.partition_broadcast(128), writes=[bflg])
    dgr = ring(L0, "dgr", [8, 8], F32, 2)
    smallr = ring(L0, "smallr", [128, 128], F32, 1)

    def alloc_mlstm(es, deep=False):
        nd = 3 if deep else 2
        T.ktokr = ring(es, "ktok", [128, 512], BF16, nd)
        T.kTr = ring(es, "kT", [128, 512], BF16, nd)
        T.qtokr = ring(es, "qtok", [128, 512], BF16, 1)
        T.qTr = ring(es, "qT", [128, 512], BF16, nd)
        T.vaugr = ring(es, "vaug", [128, 4, 129], BF16, nd)
        for t in T.vaugr.tiles:
            dve(lambda E: E.memset(t[:], 1.0), [], T.vaugr.bufs)
        T.gtr = ring(es, "gtr", [128, 96], F32, 3)
        T.rowr = ring(es, "rowr", [8, 256], F32, 3)
        T.urr = ring(es, "urr", [8, 256], F32, 2)
        T.DTr = ring(es, "DTr", [128, 512], BF16, 1)
        T.PTr = ring(es, "PTr", [128, 512], BF16, 1)
        T.hnr = ring(es, "hnr", [128, 2, 129], F32, 2)
        T.wkvr = ring(es, "wkvr", [128, 129], BF16, 2)
        T.sqr = ring(es, "sqr", [128, 512], F32, 1)
        T.q8r = ring(es, "q8r", [128, 512], F32, 2)
        T.q8br = ring(es, "q8br", [128, 512], BF16, 1)
        T.rtr = ring(es, "rtr", [128, 2, 64], F32, 2)

    def alloc_F(es):
        alloc_mlstm(es, deep=True)
        T.bph = False
        T.modg = sb(es, "modg", [128, D], F32)
        T.mods = sb(es, "mods", [128, D], F32)
        T.bmod = Buf()
        T.xr = ring(es, "xr", [128, D], F32, 2)
        T.jr = ring(es, "jr", [128, D], BF16, 1)
        T.hr = ring(es, "hr", [128, D], BF16, 1)
        T.kvr = ring(es, "kvr", [128, 272], F32, 2)
        T.cvr = ring(es, "cvr", [128, CV], BF16, 2)

    def alloc_B(es):
        alloc_mlstm(es)
        T.bph = True
        T.modgg = sb(es, "modgg", [128, D], F32)
        T.bmod = Buf()
        T.hbr = ring(es, "hbr", [128, 512], F32, 1)
        T.QTr = ring(es, "QTr", [128, 2, 512], BF16, 2)
        for t_ in T.QTr.tiles:
            dve(lambda E: E.memset(t_[64:128, 0, :], 0.0), [], T.QTr.bufs)
            dve(lambda E: E.memset(t_[0:64, 1, :], 0.0), [], T.QTr.bufs)
        T.soar = ring(es, "soar", [128, 512], BF16, 2)
        T.szr = ring(es, "szr", [128, D], BF16, 1)
        T.ypr = ring(es, "ypr", [128, D], BF16, 2)
        T.tmpr = ring(es, "tmpr", [128, 512], F32, 1)
        T.sqCr = ring(es, "sqCr", [128, 512], F32, 1)
        T.sgr = T.q8r
        T.ybr = ring(es, "ybr", [128, D], BF16, 1)
        T.yTr = ring(es, "yTr", [128, 8, 128], BF16, 1)
        T.PGr = ring(es, "PGr", [128, 512], BF16, 2)
        T.outr = ring(es, "outr", [128, 512], F32, 1)

    def prenorm(src, hT_dst, b_dst, src_reads=()):
        xt, bx = T.xr.next()
        m.dma("sp", xt[:], src, reads=list(src_reads), writes=[bx])
        st, bst = str_.next()
        jt, bj = T.jr.next()
        dve(lambda E: E.memset(st[:, 0:1], 0.0), [], [bst])
        act(lambda E: E.activation(out=jt[:], in_=xt[:], func=AF.Square, accum_out=st[:, 0:1]), [bx, bst], [bj, bst])
        rstd_from_ssq(st, bst, 1, 1.0 / D)
        ht, bh = T.hr.next()
        dve(lambda E: E.scalar_tensor_tensor(out=xt[:], in0=xt[:], scalar=st[:, 0:1], in1=T.modg[:],
                                             op0=ALU.mult, op1=ALU.mult), [bx, bst, T.bmod], [bx])
        m.op("pool", lambda E: E.tensor_tensor(out=ht[:], in0=xt[:], in1=T.mods[:], op=ALU.add), [bx, T.bmod], [bh])
        pp, pb = ps2()
        with m.group():
            for kt in range(8):
                pe(lambda E: E.matmul(pp[:, kt * 128:(kt + 1) * 128], ht[:, kt * 128:(kt + 1) * 128], identb[:],
                                      start=True, stop=True), [bh, bK], pb)
        act(lambda E: E.activation(out=hT_dst[:, 0:4, :], in_=pp[:, 0:512].rearrange("p (a b) -> p a b", a=4),
                                   func=AF.Copy), pb, [b_dst])
        dve(lambda E: E.tensor_copy(out=hT_dst[:, 4:8, :], in_=pp[:, 512:1024].rearrange("p (a b) -> p a b", a=4)),
            pb, [b_dst])

    def proj(hT, bh, c0, n):
        bank, bb = ps1()
        with m.group():
            for kt in range(8):
                pe(lambda E: E.matmul(bank[:, 0:n], hT[:, kt, :], W0[:, kt, c0:c0 + n], start=(kt == 0), stop=(kt == 7)),
                   [bh, bW0k[kt]], [bb])
        return bank, bb

    def transpose4(src, bsrc, dst, bdst, eng="act"):
        bank, bb = ps1()
        for h in range(4):
            pe(lambda E: E.matmul(bank[:, h * 128:(h + 1) * 128], src[:, h * 128:(h + 1) * 128], identb[:],
                                  start=True, stop=True), [bsrc, bK], [bb])
        if eng == "act":
            act(lambda E: E.activation(out=dst[:], in_=bank[:], func=AF.Copy), [bb], [bdst])
        else:
            dve(lambda E: E.tensor_copy(out=dst[:], in_=bank[:]), [bb], [bdst])

    def gates_prep(kvbank, bkv, slot=None):
        gt, bg = T.gtr.next()
        rw, brw = T.rowr.next()
        dg, bdg = dgr.next()
        gsrc = kvbank[:, 256:272]
        dve(lambda E: E.memset(gt[:], 0.0), [], [bg])
        dve(lambda E: E.tensor_tensor(out=gt[:, 0:16], in0=gsrc, in1=bgb[:], op=ALU.add), [bkv, bK], [bg])
        act(lambda E: E.activation(out=gt[:, 8:16], in_=gt[:, 8:16], func=AF.Exp, scale=-1.0), [bg], [bg])
        act(lambda E: E.activation(out=gt[:, 8:16], in_=gt[:, 8:16], func=AF.Ln, bias=1.0), [bg], [bg])
        dve(lambda E: E.tensor_scalar(out=gt[:, 8:16], in0=gt[:, 8:16], scalar1=-1.0, scalar2=None, op0=ALU.mult),
            [bg], [bg])
        if slot is not None:
            f = flgt[:, 0, slot * 8:(slot + 1) * 8]
            nf = flgt[:, 1, slot * 8:(slot + 1) * 8]
            dve(lambda E: E.tensor_tensor(out=gt[:, 8:16], in0=gt[:, 8:16], in1=f, op=ALU.mult), [bg, bflg], [bg])
            dve(lambda E: E.tensor_tensor(out=gt[:, 0:8], in0=gt[:, 0:8], in1=f, op=ALU.mult), [bg, bflg], [bg])
            dve(lambda E: E.tensor_tensor(out=gt[:, 0:8], in0=gt[:, 0:8], in1=nf, op=ALU.add), [bg, bflg], [bg])
        bank, bb = ps1()
        pe(lambda E: E.matmul(bank[:, 0:4], utf[:], gt[:, 8:12], start=True, stop=True), [bg, bK], [bb])
        pe(lambda E: E.matmul(bank[:, 4:8], ltf[:], gt[:, 12:16], start=True, stop=True), [bg, bK], [bb])
        pe(lambda E: E.matmul(bank[:, 8:16], onesf[:], gt[:, 8:16], start=True, stop=True), [bg, bK], [bb])
        act(lambda E: E.activation(out=gt[:, 16:32], in_=bank[:, 0:16], func=AF.Copy), [bb], [bg])
        dve(lambda E: E.tensor_tensor(out=gt[:, 32:40], in0=gt[:, 0:8], in1=gt[:, 16:24], op=ALU.subtract), [bg], [bg])
        pe(lambda E: E.matmul(bank[0:8, 128:256], gt[:, 32:40], identf[:], start=True, stop=True), [bg, bK], [bb])
        act(lambda E: E.activation(out=rw[:, 0:128], in_=bank[0:8, 128:256], func=AF.Copy), [bb], [brw])
        dve(lambda E: E.tensor_reduce(out=dg[:, 0:1], in_=rw[:, 0:128], axis=AX.X, op=ALU.max), [brw], [bdg])
        dve(lambda E: E.tensor_scalar(out=dg[:, 0:8], in0=identf[0:8, 0:8], scalar1=dg[:, 0:1], scalar2=None,
                                      op0=ALU.mult), [bdg, bK], [bdg])
        pe(lambda E: E.matmul(bank[:, 256:264], onesf[0:8, :], dg[:, 0:8], start=True, stop=True), [bdg, bK], [bb])
        act(lambda E: E.activation(out=gt[:, 40:48], in_=bank[:, 256:264], func=AF.Copy), [bb], [bg])
        return gt, bg, rw, brw

    def seq_scalar_step(d, gt, bg, mstate):
        mo, bmo = mstate[d]
        mn, bmn = mring[d].next()
        o = d * 4
        dve(lambda E: E.tensor_tensor(out=gt[:, 56 + o:60 + o], in0=mo[:], in1=gt[:, 40 + o:44 + o], op=ALU.max),
            [bmo, bg], [bg])
        dve(lambda E: E.tensor_tensor(out=mn[:], in0=gt[:, 24 + o:28 + o], in1=gt[:, 56 + o:60 + o], op=ALU.add),
            [bg], [bmn])
        dve(lambda E: E.tensor_tensor(out=gt[:, 48 + o:52 + o], in0=gt[:, 32 + o:36 + o], in1=gt[:, 56 + o:60 + o],
                                      op=ALU.subtract), [bg], [bg])
        dve(lambda E: E.tensor_tensor(out=gt[:, 56 + o:60 + o], in0=mo[:], in1=gt[:, 56 + o:60 + o], op=ALU.subtract),
            [bmo, bg], [bg])
        act(lambda E: E.activation(out=gt[:, 48 + o:52 + o], in_=gt[:, 48 + o:52 + o], func=AF.Exp), [bg], [bg])
        act(lambda E: E.activation(out=gt[:, 56 + o:60 + o], in_=gt[:, 56 + o:60 + o], func=AF.Exp), [bg], [bg])
        mstate[d] = (mn, bmn)
        return mo, bmo

    def state_update(d, h, gt, bg, ktok, bkt, vaug, bva):
        j = d * 4 + h
        wt, bwt = T.wkvr.next()
        evac(wt[:], vaug[:, h, :], [bva, bg], [bwt], scale=gt[:, 48 + j:49 + j])
        bank, bb = ps1()
        pe(lambda E: E.matmul(bank[:, 0:129], ktok[:, h * 128:(h + 1) * 128], wt[:], start=True, stop=True),
           [bkt, bwt], [bb])
        dve(lambda E: E.scalar_tensor_tensor(out=CT[:, j, :], in0=CT[:, j, :], scalar=gt[:, 56 + j:57 + j],
                                             in1=bank[:, 0:129], op0=ALU.mult, op1=ALU.add), [bCT[j], bg, bb], [bCT[j]])

    def full_step(d, gt, bg, rw, brw, mo, bmo, qT, bqT, kT, bkT_, ktok, bkt, vaug, bva, hdst, bhd):
        o = d * 4
        mC = mlt if d == 0 else mut
        mA = mut if d == 0 else mlt
        cbank, cb = ps1()
        for h in range(4):
            j = o + h
            pe(lambda E: E.matmul(cbank[:, h * 128:(h + 1) * 128], selj(j), rw[:, 0:128], start=True, stop=False),
               [bK, brw], [cb])
            pe(lambda E: E.matmul(cbank[:, h * 128:(h + 1) * 128], identb[:], mC[:], start=False, stop=True), [bK], [cb])
        dve(lambda E: E.tensor_reduce(out=gt[:, 64 + o:68 + o], in_=cbank.rearrange("p (a b) -> p a b", a=4),
                                      axis=AX.X, op=ALU.max), [cb], [bg])
        dve(lambda E: E.tensor_tensor(out=gt[:, 72 + o:76 + o], in0=mo[:], in1=gt[:, 64 + o:68 + o], op=ALU.max),
            [bmo, bg], [bg])
        dve(lambda E: E.tensor_scalar(out=gt[:, 72 + o:76 + o], in0=gt[:, 72 + o:76 + o], scalar1=-1.0, scalar2=None,
                                      op0=ALU.mult), [bg], [bg])
        dve(lambda E: E.tensor_tensor(out=gt[:, 80 + o:84 + o], in0=mo[:], in1=gt[:, 72 + o:76 + o], op=ALU.add),
            [bmo, bg], [bg])
        dve(lambda E: E.tensor_tensor(out=gt[:, 88 + o:92 + o], in0=gt[:, 72 + o:76 + o], in1=gt[:, 16 + o:20 + o],
                                      op=ALU.subtract), [bg], [bg])
        act(lambda E: E.activation(out=gt[:, 80 + o:84 + o], in_=gt[:, 80 + o:84 + o], func=AF.Exp), [bg], [bg])
        act(lambda E: E.activation(out=gt[:, 88 + o:92 + o], in_=gt[:, 88 + o:92 + o], func=AF.Exp), [bg], [bg])
        ub, ubb = ps1()
        pe(lambda E: E.matmul(ub[0:8, 0:128], gt[:, 72:80], identf[:], start=True, stop=True), [bg, bK], [ubb])
        ur, bur = T.urr.next()
        evac(ur[:, 128:256], ub[0:8, 0:128], [ubb], [bur])
        abank, ab = ps1()
        for h in range(4):
            j = o + h
            pe(lambda E: E.matmul(abank[:, h * 128:(h + 1) * 128], rw[:, 0:128], selj(j), start=True, stop=False),
               [brw, bK], [ab])
            pe(lambda E: E.matmul(abank[:, h * 128:(h + 1) * 128], selj(j), ur[:, 128:256], start=False, stop=False),
               [bur, bK], [ab])
            pe(lambda E: E.matmul(abank[:, h * 128:(h + 1) * 128], identb[:], mA[:], start=False, stop=True), [bK], [ab])
        DT, bDT = T.DTr.next()
        act(lambda E: E.activation(out=DT[:], in_=abank[:], func=AF.Exp), [ab], [bDT])
        sbank, sbb = ps1()
        for h in range(4):
            pe(lambda E: E.matmul(sbank[:, h * 128:(h + 1) * 128], kT[:, h * 128:(h + 1) * 128],
                                  qT[:, h * 128:(h + 1) * 128], start=True, stop=True), [bkT_, bqT], [sbb])
        PT, bPT = T.PTr.next()
        dve(lambda E: E.tensor_tensor(out=PT[:], in0=DT[:], in1=sbank[:], op=ALU.mult), [bDT, sbb], [bPT])
        for hp in range(2):
            ib, ibb = ps1()
            nb, nbb = ps1()
            for hh in range(2):
                h = hp * 2 + hh
                j = o + h
                m.op("pool", lambda E: E.tensor_copy(out=CTb[:, j, :], in_=CT[:, j, :]), [bCT[j]], [bCTb[j]])
                pe(lambda E: E.matmul(ib[:, hh * 129:(hh + 1) * 129], qT[:, h * 128:(h + 1) * 128], CTb[:, j, :],
                                      start=True, stop=True), [bqT, bCTb[j]], [ibb])
                pe(lambda E: E.matmul(nb[:, hh * 129:(hh + 1) * 129], PT[:, h * 128:(h + 1) * 128], vaug[:, h, :],
                                      start=True, stop=True), [bPT, bva], [nbb])
            hn, bhn = T.hnr.next()
            for hh in range(2):
                h = hp * 2 + hh
                j = o + h
                evac(hn[:, hh, :], ib[:, hh * 129:(hh + 1) * 129], [ibb, bg], [bhn], scale=gt[:, 80 + j:81 + j])
            dve(lambda E: E.tensor_tensor(out=hn[:], in0=hn[:], in1=nb[:, 0:258].rearrange("p (a b) -> p a b", a=2),
                                          op=ALU.add), [bhn, nbb], [bhn])
            st, bst = str_.next()
            dve(lambda E: E.scalar_tensor_tensor(out=st[:, 0:2], in0=hn[:, :, 128], scalar=-1.0, in1=hn[:, :, 128],
                                                 op0=ALU.mult, op1=ALU.max), [bhn], [bst])
            dve(lambda E: E.tensor_tensor(out=st[:, 0:2], in0=st[:, 0:2], in1=gt[:, 88 + o + hp * 2:90 + o + hp * 2],
                                          op=ALU.max), [bst, bg], [bst])
            dve(lambda E: E.reciprocal(out=st[:, 0:2], in_=st[:, 0:2]), [bst], [bst])
            dve(lambda E: E.tensor_tensor(out=hdst[:, hp * 256:(hp + 1) * 256].rearrange("p (a b) -> p a b", a=2),
                                          in0=hn[:, :, 0:128], in1=st[:, 0:2].unsqueeze(2).to_broadcast([128, 2, 128]),
                                          op=ALU.mult), [bhn, bst], [bhd])
        for h in range(4):
            state_update(d, h, gt, bg, ktok, bkt, vaug, bva)

    def rope(src, bsrc, nh, rope_idx, dst, bdst):
        rt, brt = T.rtr.next()
        m.dma("sp", rt[:, 0, :], ropec[rope_idx], writes=[brt])
        m.dma("sp", rt[:, 1, :], ropes[rope_idx], writes=[brt])
        sq, bsq = T.sqr.next()
        n = nh * 64
        x3 = src[:, 0:n].rearrange("p (h e) -> p h e", h=nh)
        dve(lambda E: E.tensor_tensor(out=sq[:, 0:n].rearrange("p (h e) -> p h e", h=nh), in0=x3,
                                      in1=rt[:, 0, :].unsqueeze(1).to_broadcast([128, nh, 64]), op=ALU.mult),
            [bsrc, brt], [bsq])
        x5 = src[:, 0:n].rearrange("p (h r f e) -> p h r f e", h=nh, r=2, f=2)
        tmp, btmp = T.q8r.next()
        o5 = tmp[:, 0:n].rearrange("p (h r f e) -> p h r f e", h=nh, r=2, f=2)
        s4 = rt[:, 1, :].rearrange("p (r f e) -> p r f e", r=2, f=2)
        for f in range(2):
            m.op("pool", lambda E: E.tensor_tensor(out=o5[:, :, :, f, :], in0=x5[:, :, :, 1 - f, :],
                                                   in1=s4[:, :, f, :].unsqueeze(1).to_broadcast([128, nh, 2, 16]),
                                                   op=ALU.mult), [bsrc, brt], [btmp])
        dve(lambda E: E.tensor_tensor(out=dst[:, 0:n], in0=sq[:, 0:n], in1=tmp[:, 0:n], op=ALU.add), [bsq, btmp], [bdst])

    def headnorm(x, bx, nh, hd, gain_ap, sq_ring=None):
        sq, bsq = (sq_ring or T.sqr).next()
        st, bst = str_.next()
        n = nh * hd
        dve(lambda E: E.tensor_tensor(out=sq[:, 0:n], in0=x[:, 0:n], in1=x[:, 0:n], op=ALU.mult), [bx], [bsq])
        dve(lambda E: E.tensor_reduce(out=st[:, 0:nh], in_=sq[:, 0:n].rearrange("p (a b) -> p a b", a=nh), axis=AX.X,
                                      op=ALU.add), [bsq], [bst])
        rstd_from_ssq(st, bst, nh, 1.0 / hd)
        x3 = x[:, 0:n].rearrange("p (a b) -> p a b", a=nh)
        dve(lambda E: E.tensor_tensor(out=x3, in0=x3, in1=st[:, 0:nh].unsqueeze(2).to_broadcast([128, nh, hd]),
                                      op=ALU.mult), [bx, bst], [bx])
        dve(lambda E: E.tensor_tensor(out=x3, in0=x3, in1=gain_ap, op=ALU.mult), [bx, bK], [bx])

    def kv_process(kvbank, bkv, tile_idx, rope_idx, out_seq=None, out_half=None):
        kv, bkvt = T.kvr.next()
        act(lambda E: E.activation(out=kv[:, 0:272], in_=kvbank[:, 0:272], func=AF.Copy), [bkv], [bkvt])
        headnorm(kv, bkvt, 2, 64, gqkb[:, 1, :].unsqueeze(1).to_broadcast([128, 2, 64]))
        if out_seq is not None:
            for g in range(2):
                m.dma("sp", o_gk[out_seq, g, out_half * 128:(out_half + 1) * 128, :], kv[:, g * 64:(g + 1) * 64],
                      reads=[bkvt])
                m.dma("sp", o_gv[out_seq, g, out_half * 128:(out_half + 1) * 128, :],
                      kv[:, 128 + g * 64:128 + (g + 1) * 64], reads=[bkvt])
        kb16, bk16 = T.q8br.next()
        if rope_idx is not None:
            rope(kv, bkvt, 2, rope_idx, kb16, bk16)
        else:
            dve(lambda E: E.tensor_copy(out=kb16[:, 0:128], in_=kv[:, 0:128]), [bkvt], [bk16])
        bank, bb = ps1()
        pe(lambda E: E.matmul(bank[:, 0:128], kb16[:, 0:128], identb[:], start=True, stop=True), [bk16, bK], [bb])
        act(lambda E: E.activation(out=KTg[:, tile_idx * 128:(tile_idx + 1) * 128], in_=bank[:, 0:128], func=AF.Copy),
            [bb], [bKT[tile_idx]])
        dve(lambda E: E.tensor_copy(out=Vg[:, tile_idx, :, 0:64], in_=kv[:, 128:256].rearrange("p (a b) -> p a b", a=2)),
            [bkvt], [bVg[tile_idx]])
        return kv, bkvt

    def mlstm_inputs(hT, bh, need_q):
        kbank, kbb = proj(hT, bh, C_KA, 512)
        ktok, bkt = T.ktokr.next()
        evac(ktok[:], kbank[:], [kbb], [bkt], scale=128.0 ** -0.5)
        vbank, vbb = proj(hT, bh, C_VA, 512)
        vaug, bva = T.vaugr.next()
        evac(vaug[:, :, 0:128], vbank.rearrange("p (a b) -> p a b", a=4), [vbb], [bva])
        if not need_q:
            return ktok, bkt, vaug, bva
        kT, bkT_ = T.kTr.next()
        transpose4(ktok, bkt, kT, bkT_, "dve")
        qbank, qbb = proj(hT, bh, C_QA, 512)
        qtok, bqt = T.qtokr.next()
        evac(qtok[:], qbank[:], [qbb], [bqt])
        qT, bqT = T.qTr.next()
        transpose4(qtok, bqt, qT, bqT, "dve" if T.bph else "act")
        return ktok, bkt, vaug, bva, kT, bkT_, qT, bqT

    def reset_state(d, mstate):
        dve(lambda E: E.memset(CT[:, d * 4:(d + 1) * 4, :], 0.0), [], bCT[d * 4:(d + 1) * 4])
        mt, bm = mring[d].next()
        dve(lambda E: E.memset(mt[:], 0.0), [], [bm])
        mstate[d] = (mt, bm)

    def write_state(seq_i, d, mstate):
        for h in range(4):
            j = d * 4 + h
            bank, bb = ps1()
            pe(lambda E: E.matmul(bank[:, 0:128], CT[:, j, 0:128], identf[:], start=True, stop=True), [bCT[j], bK], [bb])
            sm, bsm = smallr.next()
            act(lambda E: E.activation(out=sm[:], in_=bank[:, 0:128], func=AF.Copy), [bb], [bsm])
            m.dma("sp", o_C[seq_i, j], sm[:], reads=[bsm])
            m.dma("sp", o_n[seq_i, j].rearrange("(k o) -> k o", o=1), CT[:, j, 128:129], reads=[bCT[j]])
        mt, bm = mstate[d]
        m.dma("sp", o_m[seq_i, d * 4:(d + 1) * 4].rearrange("(o k) -> o k", o=1), mt[0:1, :], reads=[bm])

    def run_sequence(chunks, is_prompt, seq_i, n_out_slots, var, x1_base):
        n = len(chunks)
        nkt = n if is_prompt else NKT
        mstate = [None, None]
        fs = ExitStack()
        alloc_F(fs)
        m.dma("sp", T.modg[:], s_mod[0, var, 0].partition_broadcast(128), reads=[bK], writes=[T.bmod])
        m.dma("sp", T.mods[:], s_mod[0, var, 1].partition_broadcast(128), reads=[bK], writes=[T.bmod])
        if is_prompt:
            dve(lambda E: E.memset(CT[:], 0.0), [], bCT)
            for d in range(2):
                mt, bm = mring[d].next()
                dve(lambda E: E.memset(mt[:], 0.0), [], [bm])
                mstate[d] = (mt, bm)
        else:
            m.dma("sp", CT[:], ct0.rearrange("j k c -> k j c"), writes=bCT)
            for d in range(2):
                mt, bm = mring[d].next()
                m.dma("sp", mt[:], m0[d * 4:(d + 1) * 4].partition_broadcast(128), writes=[bm])
                mstate[d] = (mt, bm)
            m.dma("pool", KTg[:, 32 * 128:34 * 128], gkc, writes=[bKT[32], bKT[33]])
            m.dma("pool", Vg[:, 32:34, :, :], gvc.rearrange("t p g c -> p t g c"), writes=[bVg[32], bVg[33]])
        items = [("out", s_) for s_ in range(n_out_slots)] + [("fwd", i) for i in range(n)]
        H = {}

        def hT_of(it):
            kind, k = it
            return (hTs[:, k % 2], bhT[k % 2]) if kind == "out" else (hTs[:, k], bhT[k])

        def P1(it):
            kind, k = it
            hT, bh = hT_of(it)
            prenorm(xin[NPC + k] if kind == "out" else xin[chunks[k]], hT, bh)
            if not is_prompt:
                conv_out()
                conv_out()
                conv_in(T.cvr)
                conv_in(T.cvr)

        def P2(it):
            kind, k = it
            hT, bh = hT_of(it)
            h_ = {}
            if kind == "out":
                h_["mi"] = mlstm_inputs(hT, bh, False)
                kvbank, bkv = proj(hT, bh, C_KV, 272)
                h_["kv"] = kv_process(kvbank, bkv, k, k)
            else:
                h_["mi"] = mlstm_inputs(hT, bh, True)
                kvbank, bkv = proj(hT, bh, C_KV, 272)
                tile_idx = k if is_prompt else n_out_slots + k
                h_["kv"] = kv_process(kvbank, bkv, tile_idx, None if is_prompt else n_out_slots + k,
                                      out_seq=(seq_i + k // 2) if is_prompt else None, out_half=k % 2)
            H[it] = h_

        def P3(it):
            kind, k = it
            kvt, bkvt = H[it]["kv"]
            H[it]["g"] = gates_prep(kvt, bkvt, slot=k if kind == "out" else None)

        def Cst(it):
            kind, k = it
            h_ = H.pop(it)
            gt, bg, rw, brw = h_["g"]
            if kind == "out":
                ktok, bkt, vaug, bva = h_["mi"]
                for d in range(2):
                    seq_scalar_step(d, gt, bg, mstate)
                    for h in range(4):
                        state_update(d, h, gt, bg, ktok, bkt, vaug, bva)
            else:
                ktok, bkt, vaug, bva, kT, bkT_, qT, bqT = h_["mi"]
                if is_prompt and k % 2 == 0 and k > 0:
                    reset_state(0, mstate)
                mo, bmo = seq_scalar_step(0, gt, bg, mstate)
                full_step(0, gt, bg, rw, brw, mo, bmo, qT, bqT, kT, bkT_, ktok, bkt, vaug, bva, hfs[:, k, :], bhf[k])
                if is_prompt and k % 2 == 1:
                    write_state(seq_i + k // 2, 0, mstate)

        N = len(items)
        for t in range(N + 3):
            tasks = []
            if t < N:
                tasks.append(rec_task("a", P1, items[t]))
            if 0 <= t - 1 < N:
                tasks.append(rec_task("b", P2, items[t - 1]))
            if 0 <= t - 2 < N:
                tasks.append(rec_task("d", P3, items[t - 2]))
            if 0 <= t - 3 < N:
                tasks.append(rec_task("c", Cst, items[t - 3]))
            m.replay(tasks)
        if not is_prompt:
            while conv_state["pend"] or conv_state["in"] < len(conv_chunks):
                conv_out()
                conv_in(T.cvr)
        m.barrier()
        fs.close()
        if stage < 2:
            return
        bs_ = ExitStack()
        alloc_B(bs_)
        m.dma("sp", T.modgg[:], s_mod[0, var, 2].partition_broadcast(128), reads=[bK], writes=[T.bmod])
        order = [1, 0, 3, 2][:n] if is_prompt else list(range(n - 1, -1, -1))
        HB = {}

        def PB(i):
            hT, bh = hTs[:, i], bhT[i]
            h_ = {}
            h_["mi"] = mlstm_inputs(hT, bh, True)
            kvbank, bkv = proj(hT, bh, C_KV, 272)
            h_["g"] = gates_prep(kvbank, bkv)
            obank, obb = proj(hT, bh, C_OA, 512)
            soa, bsoa = T.soar.next()
            sgt, bsgt = T.sgr.next()
            act(lambda E: E.activation(out=sgt[:], in_=obank[:], func=AF.Exp, scale=-1.0), [obb], [bsgt])
            dve(lambda E: E.tensor_scalar(out=sgt[:], in0=sgt[:], scalar1=1.0, scalar2=None, op0=ALU.add), [bsgt], [bsgt])
            dve(lambda E: E.reciprocal(out=soa[:], in_=sgt[:]), [bsgt], [bsoa])
            h_["soa"] = (soa, bsoa)
            qbank, qbb = proj(hT, bh, C_QB, 512)
            q8, bq8 = T.q8r.next()
            evac(q8[:], qbank[:], [qbb], [bq8])
            headnorm(q8, bq8, 8, 64, gqkb[:, 0, :].unsqueeze(1).to_broadcast([128, 8, 64]))
            q16, bq16 = T.q8br.next()
            if is_prompt:
                dve(lambda E: E.tensor_copy(out=q16[:], in_=q8[:]), [bq8], [bq16])
            else:
                rope(q8, bq8, 8, n_out_slots + i, q16, bq16)
            tb, tbb = ps1()
            for p in range(4):
                pe(lambda E: E.matmul(tb[:, p * 128:(p + 1) * 128], q16[:, p * 128:(p + 1) * 128], identb[:],
                                      start=True, stop=True), [bq16, bK], [tbb])
            QT, bQT = T.QTr.next()
            evac(QT[0:64, 0, :], tb[0:64, :], [tbb], [bQT])
            evac(QT[64:128, 1, :], tb[64:128, :], [tbb], [bQT])
            h_["QT"] = (QT, bQT)
            h_["yp"] = T.ypr.next()
            HB[i] = h_

        def CB(i):
            h_ = HB[i]
            gt, bg, rw, brw = h_["g"]
            ktok, bkt, vaug, bva, kT, bkT_, qT, bqT = h_["mi"]
            soa, bsoa = h_["soa"]
            yp_, byp = h_["yp"]
            if is_prompt and i == 3:
                reset_state(1, mstate)
            mo, bmo = seq_scalar_step(1, gt, bg, mstate)
            hb, bhb = T.hbr.next()
            full_step(1, gt, bg, rw, brw, mo, bmo, qT, bqT, kT, bkT_, ktok, bkt, vaug, bva, hb, bhb)
            dve(lambda E: E.tensor_tensor(out=hb[:], in0=hb[:], in1=hfs[:, i, :], op=ALU.add), [bhb, bhf[i]], [bhb])
            headnorm(hb, bhb, 4, 128, ghnb[:].rearrange("p (a b) -> p a b", a=4), sq_ring=T.sqCr)
            dve(lambda E: E.tensor_tensor(out=yp_[:, 0:512], in0=hb[:], in1=soa[:], op=ALU.mult), [bhb, bsoa], [byp])
            if is_prompt and i % 2 == 0:
                write_state(seq_i + i // 2, 1, mstate)

        def GB(i):
            h_ = HB[i]
            QT, bQT = h_["QT"]
            yp_, byp = h_["yp"]
            Ob = [pst[3][:, 0:512], pst[3][:, 512:1024]]
            Obb = [psb[6], psb[7]]
            kts = [2 * (i // 2), 2 * (i // 2) + 1] if is_prompt else list(range(nkt))
            its = [(kt, g) for kt in kts for g in range(2)]
            pend = None
            for it in its + [None]:
                cur = None
                if it is not None:
                    kt, g = it
                    sb_, sbb = ps1()
                    pe(lambda E: E.matmul(sb_[:], KTg[:, kt * 128:(kt + 1) * 128], QT[:, g, :], start=True, stop=True),
                       [bKT[kt], bQT], [sbb])
                    PG, bPG = T.PGr.next()
                    act(lambda E: E.activation(out=PG[:], in_=sb_[:], func=AF.Exp, scale=0.125), [sbb], [bPG])
                    cur = (kt, g, PG, bPG)
                if pend is not None:
                    kt0, g0, PG0, bPG0 = pend
                    grp = m.group()
                    grp.__enter__()
                    for p in range(4):
                        pe(lambda E: E.matmul(Ob[g0][:, p * 65:(p + 1) * 65], PG0[:, p * 128:(p + 1) * 128],
                                              Vg[:, kt0, g0, :], start=(kt0 == kts[0] and p == 0), stop=(kt0 == kts[-1]),
                                              skip_group_check=True), [bPG0, bVg[kt0]], [Obb[g0]])
                    grp.__exit__(None, None, None)
                pend = cur
            for g in range(2):
                st, bst = str_.next()
                o3 = Ob[g][:, 0:260].rearrange("p (a b) -> p a b", a=4)
                dve(lambda E: E.reciprocal(out=st[:, 0:4], in_=o3[:, :, 64]), [Obb[g]], [bst])
                dve(lambda E: E.tensor_tensor(out=yp_[:, 512 + g * 256:768 + g * 256].rearrange("p (a b) -> p a b", a=4),
                                              in0=o3[:, :, 0:64], in1=st[:, 0:4].unsqueeze(2).to_broadcast([128, 4, 64]),
                                              op=ALU.mult), [Obb[g], bst], [byp])

        def FB(i):
            h_ = HB.pop(i)
            yp_, byp = h_["yp"]
            hT, bh = hTs[:, i], bhT[i]
            sz, bsz = T.szr.next()
            for hf in range(2):
                zbank, zbb = proj(hT, bh, C_Z + hf * 512, 512)
                silu_from_psum(zbank, zbb, sz[:, hf * 512:(hf + 1) * 512], bsz)
            yb, byb = T.ybr.next()
            dve(lambda E: E.tensor_tensor(out=yb[:], in0=yp_[:], in1=sz[:], op=ALU.mult), [byp, bsz], [byb])
            pp, pb = ps2()
            for kt in range(8):
                pe(lambda E: E.matmul(pp[:, kt * 128:(kt + 1) * 128], yb[:, kt * 128:(kt + 1) * 128], identb[:],
                                      start=True, stop=True), [byb, bK], pb)
            yT, byT = T.yTr.next()
            act(lambda E: E.activation(out=yT[:, 0:4, :], in_=pp[:, 0:512].rearrange("p (a b) -> p a b", a=4),
                                       func=AF.Copy), pb, [byT])
            dve(lambda E: E.tensor_copy(out=yT[:, 4:8, :], in_=pp[:, 512:1024].rearrange("p (a b) -> p a b", a=4)),
                pb, [byT])
            po, pob = ps2()
            for hf in range(2):
                for kt in range(8):
                    pe(lambda E: E.matmul(po[:, hf * 512:(hf + 1) * 512], yT[:, kt, :], W0o[:, kt, hf * 512:(hf + 1) * 512],
                                          start=(kt == 0), stop=(kt == 7)), [byT, bW0ok[kt]], [pob[hf]])
            residual_out(po, pob, xin[chunks[i]], s_x1[x1_base + i], yb, byb)

        def FP(t):
            if 0 <= t - 2 < n:
                FB(order[t - 2])
            if t < n:
                PB(order[t])

        for t in range(n + 2):
            tasks = [rec_task("a", FP, t)]
            if 0 <= t - 1 < n:
                tasks.append(rec_task("b", CB, order[t - 1]))
                tasks.append(rec_task("c", GB, order[t - 1]))
            m.replay(tasks)
        m.barrier()
        bs_.close()

    def residual_out(po, pob, src, dst, junk, bjunk, src_reads=(), final=False):
        st, bst = str_.next()
        dve(lambda E: E.memset(st[:, 0:1], 0.0), [], [bst])
        act(lambda E: E.activation(out=junk[:], in_=po[:], func=AF.Square, accum_out=st[:, 0:1]), pob + [bst],
            [bjunk, bst])
        rstd_from_ssq(st, bst, 1, 1.0 / D)
        w = T.outr.tiles[0].shape[1]
        for c0 in range(0, D, w):
            xo, bxo = T.outr.next()
            m.dma("sp", xo[:], src[:, c0:c0 + w], reads=list(src_reads), writes=[bxo])
            yp_, byp = T.tmpr.next()
            dve(lambda E: E.scalar_tensor_tensor(out=yp_[:], in0=po[:, c0:c0 + w], scalar=st[:, 0:1],
                                                 in1=T.modgg[:, c0:c0 + w], op0=ALU.mult, op1=ALU.mult),
                pob + [bst, T.bmod], [byp])
            m.op("pool", lambda E: E.tensor_tensor(out=xo[:], in0=xo[:], in1=yp_[:], op=ALU.add), [bxo, byp], [bxo])
            m.dma("sp", dst[:, c0:c0 + w], xo[:], reads=[bxo], writes=[] if final else [bX1])


    def layer1():
        if conv_state["in"] < len(conv_chunks):
            with ExitStack() as ces:
                cr = ring(ces, "cvr1", [128, CV], BF16, 4)
                while conv_state["pend"] or conv_state["in"] < len(conv_chunks):
                    conv_out()
                    conv_in(cr)
                m.barrier()
        L1 = ExitStack()
        hT1 = sb(L1, "hT1", [128, NIN, 8, 128], BF16)
        bh1 = [Buf() for _ in range(NIN)]
        K1T = sb(L1, "K1T", [128, 8, NIN * 128], BF16)
        bK1 = [Buf() for _ in range(NIN)]
        V1 = sb(L1, "V1", [128, NIN, 16, 65], BF16)
        bV1 = [Buf() for _ in range(NIN)]
        for i_ in range(NIN):
            dve(lambda E: E.memset(V1[:, i_], 1.0), [], [bV1[i_]])
        st1 = ring(L1, "st1", [128, 16], F32, 6)

        E2 = sb(L1, "E2", [128, 16, 14, 64], BF16)
        bE2 = Buf()
        Kc = sb(L1, "Kc", [128, 8, 256], BF16)
        Vc = sb(L1, "Vc", [128, 2, 16, 65], BF16)
        bKc = Buf()
        m.dma("pool", Kc[:], nkc, writes=[bKc])
        m.dma("pool", Vc[:], nvc.rearrange("t p h c -> p t h c"), writes=[bKc])
        with ExitStack() as es2:
            ebr = ring(es2, "ebr", [128, 2, 14, 64], F32, 2)
            for hq in range(8):
                et, bet = ebr.next()
                m.dma("sp", et[:], ebias[:, hq * 2:(hq + 1) * 2], writes=[bet])
                act(lambda E: E.activation(out=E2[:, hq * 2:(hq + 1) * 2], in_=et[:], func=AF.Exp), [bet], [bE2])
            m.barrier()

        def pass1(idxs, is_prompt, var):
            es = ExitStack()
            Wa = sb(es, "W1a", [128, 8, 2048], BF16)
            bWak = [Buf() for _ in range(8)]
            for kt in range(8):
                m.dma("sp", Wa[:, kt, :], s_w1[:, kt * 4096 + 1024:kt * 4096 + 3072], reads=bWscs, writes=[bWak[kt]])
            T.modg = sb(es, "modg1", [128, D], F32)
            T.mods = sb(es, "mods1", [128, D], F32)
            T.bmod = Buf()
            m.dma("sp", T.modg[:], s_mod[1, var, 0].partition_broadcast(128), reads=[bK], writes=[T.bmod])
            m.dma("sp", T.mods[:], s_mod[1, var, 1].partition_broadcast(128), reads=[bK], writes=[T.bmod])
            T.xr = ring(es, "xr1", [128, D], F32, 2)
            T.jr = ring(es, "jr1", [128, D], BF16, 1)
            T.hr = ring(es, "hr1", [128, D], BF16, 1)
            k32r = ring(es, "k32r", [128, D], F32, 2)
            k16r = ring(es, "k16r", [128, D], BF16, 2)
            v32r = ring(es, "v32r", [128, D], F32, 2) if is_prompt else k32r

            def A1(i):
                prenorm(s_x1[idxs[i]], hT1[:, i], bh1[i], src_reads=[bX1])

            def A2(i):
                hT, bh = hT1[:, i], bh1[i]
                seq_i, half = i // 2, i % 2
                k32, bk32 = k32r.next()
                k16, bk16 = k16r.next()
                for hf in range(2):
                    bank, bb = ps1()
                    with m.group():
                        for kt in range(8):
                            pe(lambda E: E.matmul(bank[:], hT[:, kt, :], Wa[:, kt, hf * 512:(hf + 1) * 512],
                                                  start=(kt == 0), stop=(kt == 7)), [bh, bWak[kt]], [bb])
                    if is_prompt:
                        act(lambda E: E.activation(out=k32[:, hf * 512:(hf + 1) * 512], in_=bank[:], func=AF.Copy),
                            [bb], [bk32])
                        m.op("pool", lambda E: E.tensor_copy(out=k16[:, hf * 512:(hf + 1) * 512],
                                                             in_=k32[:, hf * 512:(hf + 1) * 512]), [bk32], [bk16])
                    else:
                        dve(lambda E: E.tensor_copy(out=k16[:, hf * 512:(hf + 1) * 512], in_=bank[:]), [bb], [bk16])
                if is_prompt:
                    for h in range(16):
                        m.dma("sp", o_nk[seq_i, h, half * 128:(half + 1) * 128, :], k32[:, h * 64:(h + 1) * 64],
                              reads=[bk32])
                pp, pb = ps2()
                for hp in range(8):
                    pe(lambda E: E.matmul(pp[:, hp * 128:(hp + 1) * 128], k16[:, hp * 128:(hp + 1) * 128], identb[:],
                                          start=True, stop=True), [bk16, bK], pb)
                act(lambda E: E.activation(out=K1T[:, 0:4, i * 128:(i + 1) * 128],
                                           in_=pp[:, 0:512].rearrange("p (a b) -> p a b", a=4), func=AF.Copy),
                    pb, [bK1[i]])
                dve(lambda E: E.tensor_copy(out=K1T[:, 4:8, i * 128:(i + 1) * 128],
                                            in_=pp[:, 512:1024].rearrange("p (a b) -> p a b", a=4)), pb, [bK1[i]])
            def A3(i):
                hT, bh = hT1[:, i], bh1[i]
                seq_i, half = i // 2, i % 2
                v32, bv32 = v32r.next()
                for hf in range(2):
                    bank, bb = ps1()
                    with m.group():
                        for kt in range(8):
                            pe(lambda E: E.matmul(bank[:], hT[:, kt, :], Wa[:, kt, 1024 + hf * 512:1024 + (hf + 1) * 512],
                                                  start=(kt == 0), stop=(kt == 7)), [bh, bWak[kt]], [bb])
                    if is_prompt:
                        act(lambda E: E.activation(out=v32[:, hf * 512:(hf + 1) * 512], in_=bank[:], func=AF.Copy),
                            [bb], [bv32])
                        m.op("pool", lambda E: E.tensor_copy(out=V1[:, i, hf * 8:(hf + 1) * 8, 0:64],
                                                             in_=v32[:, hf * 512:(hf + 1) * 512].rearrange("p (h e) -> p h e", h=8)),
                             [bv32], [bV1[i]])
                    else:
                        dve(lambda E: E.tensor_copy(out=V1[:, i, hf * 8:(hf + 1) * 8, 0:64],
                                                    in_=bank.rearrange("p (h e) -> p h e", h=8)), [bb], [bV1[i]])
                if is_prompt:
                    for h in range(16):
                        m.dma("sp", o_nv[seq_i, h, half * 128:(half + 1) * 128, :], v32[:, h * 64:(h + 1) * 64],
                              reads=[bv32])

            n_ = len(idxs)
            for t in range(n_ + 1):
                tasks = []
                if t < n_:
                    tasks.append(rec_task("a", A1, t))
                if 0 <= t - 1 < n_:
                    tasks.append(rec_task("b", A2, t - 1))
                    tasks.append(rec_task("c", A3, t - 1))
                m.replay(tasks)
            m.barrier()
            es.close()

        def pass2(idxs, is_prompt, var, odst):
            es = ExitStack()
            Wb = sb(es, "W1b", [128, 8, 2048], BF16)
            Wo = sb(es, "W1o", [128, 8, D], BF16)
            bWbq = [Buf() for _ in range(8)]; bWbz = [Buf() for _ in range(8)]; bWo = Buf()
            for kt in range(8):
                m.dma("sp", Wb[:, kt, 0:1024], s_w1[:, kt * 4096:kt * 4096 + 1024], reads=bWscs, writes=[bWbq[kt]])
                m.dma("sp", Wb[:, kt, 1024:2048], s_w1[:, kt * 4096 + 3072:kt * 4096 + 4096], reads=bWscs, writes=[bWbz[kt]])
            m.dma("sp", Wo[:].rearrange("p k c -> p (k c)"), s_w1o, reads=bWscs, writes=[bWo])
            T.modgg = sb(es, "modgg1", [128, D], F32)
            T.bmod = Buf()
            m.dma("sp", T.modgg[:], s_mod[1, var, 2].partition_broadcast(128), reads=[bK], writes=[T.bmod])
            T.outr = ring(es, "outr1", [128, 512], F32, 2)
            T.tmpr = ring(es, "tmpr1", [128, 512], F32, 1)
            q16r = ring(es, "q16r", [128, D], BF16, 1)
            Q1Tr = ring(es, "Q1Tr", [128, 2, 8, 128], BF16, 2)
            for t_ in Q1Tr.tiles:
                dve(lambda E: E.memset(t_[64:128, 0], 0.0), [], Q1Tr.bufs)
                dve(lambda E: E.memset(t_[0:64, 1], 0.0), [], Q1Tr.bufs)
            szr = ring(es, "szr1", [128, D], BF16, 1)
            y1r = ring(es, "y1r", [128, D], BF16, 2)
            ybr = ring(es, "ybr1", [128, D], BF16, 1)
            yTr = ring(es, "yTr1", [128, 8, 128], BF16, 1)
            Pr = ring(es, "Pr", [128, 1024], BF16, 2) if is_prompt else ring(es, "Pr", [128, 896], BF16, 3)
            HQ = {}

            def B1(i):
                hT, bh = hT1[:, i], bh1[i]
                q16, bq16 = q16r.next()
                for hf in range(2):
                    bank, bb = ps1()
                    with m.group():
                        for kt in range(8):
                            pe(lambda E: E.matmul(bank[:], hT[:, kt, :], Wb[:, kt, hf * 512:(hf + 1) * 512],
                                                  start=(kt == 0), stop=(kt == 7)), [bh, bWbq[kt]], [bb])
                    dve(lambda E: E.tensor_copy(out=q16[:, hf * 512:(hf + 1) * 512], in_=bank[:]), [bb], [bq16])
                pp, pb = ps2()
                for hp in range(8):
                    pe(lambda E: E.matmul(pp[:, hp * 128:(hp + 1) * 128], q16[:, hp * 128:(hp + 1) * 128], identb[:],
                                          start=True, stop=True), [bq16, bK], pb)
                Q1T, bQ1 = Q1Tr.next()
                for hh_ in range(2):
                    rs = slice(hh_ * 64, hh_ * 64 + 64)
                    act(lambda E: E.activation(out=Q1T[rs, hh_, 0:4, :],
                                               in_=pp[rs, 0:512].rearrange("p (a b) -> p a b", a=4), func=AF.Copy),
                        pb, [bQ1])
                    dve(lambda E: E.tensor_copy(out=Q1T[rs, hh_, 4:8, :],
                                                in_=pp[rs, 512:1024].rearrange("p (a b) -> p a b", a=4)), pb, [bQ1])
                HQ[i] = (Q1T, bQ1, y1r.next())

            def o_post(hg, Ob, Obb, y1, by1):
                st, bst = str_.next()
                o3 = Ob[:, 0:260].rearrange("p (a b) -> p a b", a=4)
                dve(lambda E: E.reciprocal(out=st[:, 0:4], in_=o3[:, :, 64]), [Obb], [bst])
                dve(lambda E: E.tensor_tensor(out=y1[:, hg * 256:(hg + 1) * 256].rearrange("p (a b) -> p a b", a=4),
                                              in0=o3[:, :, 0:64],
                                              in1=st[:, 0:4].unsqueeze(2).to_broadcast([128, 4, 64]), op=ALU.mult),
                    [Obb, bst], [by1])

            def B2(i):
                Q1T, bQ1, (y1, by1) = HQ[i]
                if is_prompt:
                    kts = [2 * (i // 2), 2 * (i // 2) + 1]
                    for hg in range(4):
                        Ob = pst[3][:, (hg % 2) * 512:(hg % 2) * 512 + 512]
                        Obb = psb[6 + hg % 2]
                        sp_, spb = ps2()
                        for hl in range(4):
                            h = hg * 4 + hl
                            hp, hh = h // 2, h % 2
                            for kk, kt in enumerate(kts):
                                c0 = ((hl % 2) * 4 + (hl // 2) * 2 + kk) * 128
                                pe(lambda E: E.matmul(sp_[:, c0:c0 + 128], K1T[:, hp, kt * 128:(kt + 1) * 128],
                                                      Q1T[:, hh, hp, :], start=True, stop=True),
                                   [bK1[kt], bQ1], spb)
                        P, bP = Pr.next()
                        act(lambda E: E.activation(out=P[:], in_=sp_[:], func=AF.Exp, scale=0.125), spb, [bP])
                        for hl in range(4):
                            h = hg * 4 + hl
                            for kk, kt in enumerate(kts):
                                c0 = ((hl % 2) * 4 + (hl // 2) * 2 + kk) * 128
                                pe(lambda E: E.matmul(Ob[:, hl * 65:(hl + 1) * 65], P[:, c0:c0 + 128], V1[:, kt, h, :],
                                                      start=(kk == 0), stop=(kk == 1)), [bP, bV1[kt]], [Obb])
                        o_post(hg, Ob, Obb, y1, by1)
                    return
                r0 = 2 * i
                if 2 <= i <= 9:
                    tiles = [r0 - 4 + 2 * t for t in range(5)]
                else:
                    base = 0 if i < 2 else 16
                    tiles = [base + 2 * t for t in range(4)]
                nt = len(tiles)
                s0 = tiles[0] - r0 + 7
                E7 = E2[:].rearrange("p h (a b) e -> p h a b e", b=2)

                def na_stage1(u):
                    hg, hl = u
                    h = hg * 4 + hl
                    hp, hh = h // 2, h % 2
                    banks = [ps1(), ps1()]
                    P, bP = Pr.next()
                    segs = [("loc", a_) for a_ in tiles] + [("ctx", 0), ("ctx", 1)]
                    for si, (kind, a_) in enumerate(segs):
                        bank, bb = banks[si // 4]
                        c0 = (si % 4) * 128
                        if kind == "loc":
                            pe(lambda E: E.matmul(bank[:, c0:c0 + 128], K1T[:, hp, a_ * 64:a_ * 64 + 128],
                                                  Q1T[:, hh, hp, :], start=True, stop=True),
                               [bK1[a_ // 2], bQ1], [bb])
                        else:
                            pe(lambda E: E.matmul(bank[:, c0:c0 + 128], Kc[:, hp, a_ * 128:(a_ + 1) * 128],
                                                  Q1T[:, hh, hp, :], start=True, stop=True), [bKc, bQ1], [bb])
                    n0 = min(4, len(segs)) * 128
                    n1 = (len(segs) - 4) * 128
                    act(lambda E: E.activation(out=P[:, 0:n0], in_=banks[0][0][:, 0:n0], func=AF.Exp, scale=0.125),
                        [banks[0][1]], [bP])
                    act(lambda E: E.activation(out=P[:, n0:n0 + n1], in_=banks[1][0][:, 0:n1], func=AF.Exp, scale=0.125),
                        [banks[1][1]], [bP])
                    P4 = P[:, 0:nt * 128].rearrange("p (t r e) -> p t r e", t=nt, r=2)
                    for ir in range(2):
                        sl = s0 - ir
                        m.op("dve" if ir == 0 else "pool",
                             lambda E: E.tensor_tensor(out=P4[:, :, ir, :], in0=P4[:, :, ir, :],
                                                       in1=E7[:, h, sl // 2:sl // 2 + nt, sl % 2, :], op=ALU.mult),
                             [bP, bE2], [bP])
                    if nt == 5:
                        m.op("pool", lambda E: E.memset(P[0:64, 64:128], 0.0), [bP], [bP])
                        m.op("pool", lambda E: E.memset(P[:, 512:576], 0.0), [bP], [bP])
                        m.op("pool", lambda E: E.memset(P[64:128, 576:640], 0.0), [bP], [bP])
                    return (u, segs, P, bP)

                def na_stage2(rec):
                    (hg, hl), segs, P, bP = rec
                    h = hg * 4 + hl
                    Ob = pst[3][:, (hg % 2) * 512:(hg % 2) * 512 + 512]
                    Obb = psb[6 + hg % 2]
                    for si, (kind, a_) in enumerate(segs):
                        rhs = V1[:, a_ // 2, h, :] if kind == "loc" else Vc[:, a_, h, :]
                        pe(lambda E: E.matmul(Ob[:, hl * 65:(hl + 1) * 65], P[:, si * 128:(si + 1) * 128], rhs,
                                              start=(si == 0), stop=(si == len(segs) - 1)),
                           [bP, bV1[a_ // 2] if kind == "loc" else bKc], [Obb])
                    if hl == 3:
                        o_post(hg, Ob, Obb, y1, by1)

                pend = None
                for u in [(hg, hl) for hg in range(4) for hl in range(4)] + [None]:
                    cur = na_stage1(u) if u is not None else None
                    if pend is not None:
                        na_stage2(pend)
                    pend = cur

            def B3(i):
                hT, bh = hT1[:, i], bh1[i]
                Q1T, bQ1, (y1, by1) = HQ.pop(i)
                sz, bsz = szr.next()
                for hf in range(2):
                    bank, bb = ps1()
                    with m.group():
                        for kt in range(8):
                            pe(lambda E: E.matmul(bank[:], hT[:, kt, :], Wb[:, kt, 1024 + hf * 512:1024 + (hf + 1) * 512],
                                                  start=(kt == 0), stop=(kt == 7)), [bh, bWbz[kt]], [bb])
                    silu_from_psum(bank, bb, sz[:, hf * 512:(hf + 1) * 512], bsz)
                yb, byb = ybr.next()
                dve(lambda E: E.tensor_tensor(out=yb[:], in0=y1[:], in1=sz[:], op=ALU.mult), [by1, bsz], [byb])
                pp, pb = ps2()
                for kt in range(8):
                    pe(lambda E: E.matmul(pp[:, kt * 128:(kt + 1) * 128], yb[:, kt * 128:(kt + 1) * 128], identb[:],
                                          start=True, stop=True), [byb, bK], pb)
                yT, byT = yTr.next()
                act(lambda E: E.activation(out=yT[:, 0:4, :], in_=pp[:, 0:512].rearrange("p (a b) -> p a b", a=4),
                                           func=AF.Copy), pb, [byT])
                dve(lambda E: E.tensor_copy(out=yT[:, 4:8, :], in_=pp[:, 512:1024].rearrange("p (a b) -> p a b", a=4)),
                    pb, [byT])
                po, pob = ps2()
                for hf in range(2):
                    for kt in range(8):
                        pe(lambda E: E.matmul(po[:, hf * 512:(hf + 1) * 512], yT[:, kt, :], Wo[:, kt, hf * 512:(hf + 1) * 512],
                                              start=(kt == 0), stop=(kt == 7)), [byT, bWo], [pob[hf]])
                residual_out(po, pob, s_x1[idxs[i]], odst[i], yb, byb, src_reads=[bX1], final=True)

            n_ = len(idxs)
            for t in range(n_ + 2):
                tasks = []
                if t < n_:
                    tasks.append(rec_task("a", B1, t))
                if 0 <= t - 1 < n_:
                    tasks.append(rec_task("b", B2, t - 1))
                if 0 <= t - 2 < n_:
                    tasks.append(rec_task("c", B3, t - 2))
                m.replay(tasks)
            m.barrier()
            es.close()

        pass1([0, 1, 2, 3], True, 0)
        if stage >= 5:
            pass2([0, 1, 2, 3], True, 0, [o_yp[i] for i in range(4)])
        sidx = [NPC + i for i in range(NIN)]
        if stage >= 7:
            pass1(sidx, False, 1)
        if stage >= 8:
            pass2(sidx, False, 1, [o_ys[i] for i in range(NIN)])
        L1.close()


    bX1 = Buf()
    run_sequence([0, 1, 2, 3], True, 0, 0, 0, 0)
    if stage >= 3 and not no_sample:
        run_sequence([NPC + NOUT + i for i in range(NIN)], False, None, NOUT, 1, NPC)
    if stage < 4:
        tr = ring(L0, "dbgx", [128, D], F32, 2)
        for i in range(NPC):
            t, b_ = tr.next()
            m.dma("sp", t[:], s_x1[i], reads=[bX1], writes=[b_])
            m.dma("sp", o_yp[i], t[:], reads=[b_])
        if stage >= 3:
            for i in range(NIN):
                t, b_ = tr.next()
                m.dma("sp", t[:], s_x1[NPC + i], reads=[bX1], writes=[b_])
                m.dma("sp", o_ys[i], t[:], reads=[b_])
    m.barrier()
    L0.close()
    LW.close()
    if stage >= 4:
        layer1()
    m.finish()
    return nc, m


def _kt(w):
    k, n = w.shape
    return np.ascontiguousarray(w.reshape(k // 128, 128, n).transpose(1, 0, 2))


def _consts():
    i = np.arange(128)
    ut = (i[:, None] <= i[None, :]).astype(np.float32)
    lt = (i[:, None] >= i[None, :]).astype(np.float32)
    sel = np.zeros((8, 8, 128), np.float32)
    for j in range(8):
        sel[j, j, :] = 1.0
    pick = np.zeros((2, 128, 128), np.float32)
    pick[0, 127, :] = 1.0
    pick[1, 0, :] = 1.0
    return dict(cident=np.eye(128, dtype=np.float32), cut=ut, clt=lt,
                cmut=np.where(ut > 0, 0.0, NEG).astype(np.float32), cmlt=np.where(lt > 0, 0.0, NEG).astype(np.float32),
                csel=sel.reshape(8, 1024), cones=np.ones((128, 128), np.float32), cpick=pick)


def _prep(inp):
    f = lambda k: np.asarray(inp[k], dtype=np.float32)
    xp, xs = f("x_prompt"), f("x_sample")
    w_in = f("w_in_ab")[0]
    gperm = [0, 1, 2, 3, 8, 9, 10, 11, 4, 5, 6, 7, 12, 13, 14, 15]
    gates = w_in[:, 2048:2064][:, gperm]
    w0 = np.concatenate([w_in[:, 512:1024], w_in[:, 1024:1536], w_in[:, 2576:2704], w_in[:, 2704:2832], gates,
                         w_in[:, 0:512], w_in[:, 1536:2048],
                         w_in[:, 2064:2576].reshape(D, 2, 4, 64).transpose(0, 2, 1, 3).reshape(D, 512),
                         w_in[:, 2832:3856]], axis=1)
    assert w0.shape[1] == NW0
    shared = dict(
        w0=_kt(w0), w0o=_kt(f("w_out_ab")[0]), w1=_kt(f("w_in_c")[0]), w1o=_kt(f("w_out_c")[0]),
        wmod=np.stack([_kt(f("w_mod")[l]) for l in range(2)]), bmod=f("b_mod"), gpre=f("g_pre"), gpost=f("g_post"),
        bgate=f("b_gates_ab")[0][gperm], ghn=f("g_hnorm_a")[0],
        gqk=np.stack([f("g_qnorm_b")[0], f("g_knorm_b")[0]]),
    )
    shared.update(_consts())
    t = np.arange(4096)
    fr = (10000.0 ** (-np.arange(16, dtype=np.float32) / 16)).astype(np.float32)
    ar = (t // 64).astype(np.float32)[:, None] * fr
    ac = (t % 64).astype(np.float32)[:, None] * fr
    cos64 = np.concatenate([np.cos(ar), np.cos(ar), np.cos(ac), np.cos(ac)], 1).astype(np.float32)
    sin64 = np.concatenate([-np.sin(ar), np.sin(ar), -np.sin(ac), np.sin(ac)], 1).astype(np.float32)
    rpb = f("rpb_c")[0]
    ck = np.arange(64)[:, None]
    cq = np.arange(64)[None, :]
    cs = np.clip(cq - 8, 0, 48)
    valid = (ck >= cs) & (ck < cs + 16)
    dcol = np.clip(ck - cq + 15, 0, 30)
    eb = np.full((2, 64, 16, 14, 64), NEG, np.float32)
    for dr0 in range(14):
        for jr in range(2):
            eb[jr, :, :, dr0, :] = np.where(valid[:, None, :], rpb[:, dr0 + jr, :][:, dcol].transpose(1, 0, 2), NEG)
    shared["ebias"] = eb.reshape(128, 16, 14, 64)
    maps = []
    for c in range(NCORES):
        b, q = c // 4, c % 4
        c0 = STARTS[q] // 128
        inside = list(range(c0, c0 + NIN))
        before = list(range(0, c0))
        after = list(range(c0 + NIN, 32))[::-1]
        outside = before + after
        xsb = xs[b].reshape(32, 128, D)
        xin = np.concatenate([xp[2 * c].reshape(2, 128, D), xp[2 * c + 1].reshape(2, 128, D), xsb[outside], xsb[inside]], 0)
        order = outside + inside
        fl = np.zeros((NOUT, 8), np.float32)
        fl[:len(before), 0:4] = 1.0
        fl[len(before):, 4:8] = 1.0
        flg = np.stack([fl.reshape(-1), ((1.0 - fl) * -1e30).reshape(-1)]).astype(np.float32)
        Cst = f("state_mlstm_C")[b, 0]
        nst = f("state_mlstm_n")[b, 0]
        ct0 = np.concatenate([Cst.transpose(0, 1, 3, 2), nst[..., None]], -1).reshape(8, 128, 129)
        gk = f("cache_gqa_k")[b, 0]
        gv = f("cache_gqa_v")[b, 0]
        gvc = np.ones((2, 128, 2, 65), np.float32)
        gvc[..., 0:64] = gv.reshape(2, 2, 128, 64).transpose(1, 2, 0, 3)
        nk = f("cache_na_k")[b, 0]
        nv = f("cache_na_v")[b, 0]
        nvc = np.ones((2, 128, 16, 65), np.float32)
        nvc[..., 0:64] = nv.reshape(16, 2, 128, 64).transpose(1, 2, 0, 3)
        d = dict(shared)
        d.update(
            xin=np.ascontiguousarray(xin),
            cvT=np.ascontiguousarray(np.stack([f("c_ctx"), f("c")[b]], -1).reshape(8, 128, 2).transpose(1, 0, 2)),
            ropec=cos64.reshape(32, 128, 64)[order], ropes=sin64.reshape(32, 128, 64)[order],
            ct0=np.ascontiguousarray(ct0), m0=f("state_mlstm_m")[b, 0].reshape(8), flg=flg,
            gkc=np.ascontiguousarray(gk.transpose(0, 2, 1).reshape(128, 256)), gvc=gvc,
            nkc=np.ascontiguousarray(nk.reshape(8, 2, 256, 64).transpose(1, 3, 0, 2).reshape(128, 8, 256)), nvc=nvc,
        )
        maps.append({k: np.ascontiguousarray(v, dtype=np.float32) for k, v in d.items()})
    return maps


_CACHE = {}


def kernel(**inputs):
    stage = inputs.pop("_stage", 99)
    maps = _prep(inputs)
    if stage not in _CACHE:
        _CACHE[stage] = build(stage)
    nc, _ = _CACHE[stage]
    res = run_bass_kernel_spmd(nc, maps, core_ids=list(range(NCORES)))
    R = res.results
    yp = np.zeros((16, 256, D), np.float32)
    ys = np.zeros((2, 4096, D), np.float32)
    oC = np.zeros((16, 1, 2, 4, 128, 128), np.float32)
    on = np.zeros((16, 1, 2, 4, 128), np.float32)
    om = np.zeros((16, 1, 2, 4), np.float32)
    ogk = np.zeros((16, 1, 2, 256, 64), np.float32)
    ogv = np.zeros((16, 1, 2, 256, 64), np.float32)
    onk = np.zeros((16, 1, 16, 256, 64), np.float32)
    onv = np.zeros((16, 1, 16, 256, 64), np.float32)
    for c in range(NCORES):
        r = R[c]
        b, q = c // 4, c % 4
        yp[2 * c:2 * c + 2] = r["o_yp"].reshape(2, 256, D)
        loc = 1024 * q - STARTS[q]
        ys[b, 1024 * q:1024 * (q + 1)] = r["o_ys"].reshape(NIN * 128, D)[loc:loc + 1024]
        oC[2 * c:2 * c + 2, 0] = r["o_C"].reshape(2, 2, 4, 128, 128)
        on[2 * c:2 * c + 2, 0] = r["o_n"].reshape(2, 2, 4, 128)
        om[2 * c:2 * c + 2, 0] = r["o_m"].reshape(2, 2, 4)
        ogk[2 * c:2 * c + 2, 0] = r["o_gk"]
        ogv[2 * c:2 * c + 2, 0] = r["o_gv"]
        onk[2 * c:2 * c + 2, 0] = r["o_nk"]
        onv[2 * c:2 * c + 2, 0] = r["o_nv"]
    return (yp, ys, oC, on, om, ogk, ogv, onk, onv)
```
